# Optimizing a Trainium2 kernel written in Bass

```python
import jax, jax.numpy as jnp
from jax import lax
import numpy as np

D_MODEL = 2048
BATCH = 32
SEQ = 256
DEPTH = 1
DEC_BATCH = 4
DEC_SEQ = 4096
PAST_LEN = 256

GRID_W = 64
H_A = 8
Q_LORA = 512
KV_LORA = 256
NOPE_DIM = 128
ROPE_DIM = 64
V_DIM = 128
H_B = 8
HD_B = 128
NA_W = H_B * HD_B
WIN_R_MAX = 8
WIN_C = 16
COL_QBLOCK = 16
COL_KBLOCK = COL_QBLOCK + WIN_C
D_FF = -(-8 * D_MODEL // (3 * 256)) * 256
IN_COLS = Q_LORA + KV_LORA + ROPE_DIM + 3 * NA_W + 2 * D_MODEL
ROPE_THETA = 10000.0
NORM_EPS = 1e-6
Q_BLOCK = 128
NEG_INF = -1e30
MLA_SCALE = (NOPE_DIM + ROPE_DIM) ** -0.5
NA_SCALE = HD_B ** -0.5

kernel_name = 'hybrid_mla_natten_prefix_dit_step'


def rmsnorm(x, g):
    xf = x.astype(jnp.float32)
    xf = xf * lax.rsqrt(jnp.mean(xf * xf, axis=-1, keepdims=True) + NORM_EPS)
    return (xf * g.astype(jnp.float32)).astype(x.dtype)


def modulate(x, shift, scale):
    return x * (1 + scale) + shift


def adaln_params(cond, w_mod, b_mod):
    m = jax.nn.silu(cond) @ w_mod + b_mod
    return jnp.split(m, 6, axis=-1)


def axial_rope_tables(n_tokens):
    t = jnp.arange(n_tokens, dtype=jnp.int32)
    row = (t // GRID_W).astype(jnp.float32)
    col = (t % GRID_W).astype(jnp.float32)
    n_freq = ROPE_DIM // 4
    inv_freq = ROPE_THETA ** (-jnp.arange(n_freq, dtype=jnp.float32) / n_freq)
    ang = jnp.concatenate([row[:, None] * inv_freq, col[:, None] * inv_freq], axis=-1)
    return jnp.cos(ang), jnp.sin(ang)


def apply_rope(x, cos, sin):
    half = ROPE_DIM // 2
    xf = x.astype(jnp.float32)
    x1, x2 = xf[..., :half], xf[..., half:]
    return jnp.concatenate([x1 * cos - x2 * sin, x2 * cos + x1 * sin], axis=-1).astype(x.dtype)


def split_projection(p):
    sizes = [Q_LORA, KV_LORA, ROPE_DIM, NA_W, NA_W, NA_W, D_MODEL, D_MODEL]
    return jnp.split(p, [int(i) for i in np.cumsum(sizes)[:-1]], axis=-1)


def split_heads(x, n_heads):
    return x.reshape(x.shape[:-1] + (n_heads, x.shape[-1] // n_heads))


def mla_queries(c_q, q_norm_g, w_uq):
    return split_heads(rmsnorm(c_q, q_norm_g) @ w_uq, H_A)


def mla_keys_values(c_kv, k_rope, w_uk, w_uv):
    b, s, _ = c_kv.shape
    k_nope = (c_kv @ w_uk).reshape(b, s, H_A, NOPE_DIM)
    v = (c_kv @ w_uv).reshape(b, s, H_A, V_DIM)
    k = jnp.concatenate([k_nope, jnp.broadcast_to(k_rope[:, :, None, :], (b, s, H_A, ROPE_DIM))], axis=-1)
    return k, v


def dense_attention(q, k, v, scale):
    b, sq, h, dk = q.shape
    n_blk = sq // Q_BLOCK
    qb = jnp.moveaxis((q * scale).reshape(b, n_blk, Q_BLOCK, h, dk), 1, 0)

    def block(q_blk):
        logits = jnp.einsum('bqhd,bkhd->bhqk', q_blk, k).astype(jnp.float32)
        p = jax.nn.softmax(logits, axis=-1).astype(v.dtype)
        return jnp.einsum('bhqk,bkhd->bqhd', p, v)

    out = lax.map(block, qb)
    return jnp.moveaxis(out, 0, 1).reshape(b, sq, h, v.shape[-1])


def natten_col_tables():
    n_cb = GRID_W // COL_QBLOCK
    kb = np.clip(np.arange(n_cb) * COL_QBLOCK - WIN_C // 2, 0, GRID_W - COL_KBLOCK)
    col_idx = kb[:, None] + np.arange(COL_KBLOCK)[None, :]
    q_col = np.arange(GRID_W).reshape(n_cb, COL_QBLOCK)
    q_start = np.clip(q_col - WIN_C // 2, 0, GRID_W - WIN_C)
    k_col = col_idx[:, None, :]
    in_win = (k_col >= q_start[..., None]) & (k_col < q_start[..., None] + WIN_C)
    col_off = np.clip(k_col - q_col[..., None] + WIN_C - 1, 0, 2 * WIN_C - 2)
    return col_idx, in_win, col_off


def neighborhood_attention(q, k, v, k_ctx, v_ctx, rpb, rows):
    b = q.shape[0]
    win_r = min(WIN_R_MAX, rows)
    n_cb = GRID_W // COL_QBLOCK
    n_win = win_r * COL_KBLOCK
    col_idx, in_win, col_off = natten_col_tables()
    qg = (q * NA_SCALE).reshape(b, rows, GRID_W, H_B, HD_B)
    kg = k.reshape(b, rows, GRID_W, H_B, HD_B)
    vg = v.reshape(b, rows, GRID_W, H_B, HD_B)
    rpb_cols = rpb[:, :, col_off]
    mask = jnp.asarray(in_win)[:, :, None, :]

    def row_step(r):
        q_r = lax.dynamic_index_in_dim(qg, r, axis=1, keepdims=False).reshape(b, n_cb, COL_QBLOCK, H_B, HD_B)
        r0 = jnp.clip(r - win_r // 2, 0, rows - win_r)
        k_blk = lax.dynamic_slice_in_dim(kg, r0, win_r, axis=1)[:, :, col_idx]
        v_blk = lax.dynamic_slice_in_dim(vg, r0, win_r, axis=1)[:, :, col_idx]
        row_off = r0 + jnp.arange(win_r, dtype=jnp.int32) - r + (WIN_R_MAX - 1)
        bias = jnp.transpose(jnp.take(rpb_cols, row_off, axis=1), (0, 2, 3, 1, 4))
        s_win = jnp.einsum('bnqhd,brnkhd->bhnqrk', q_r, k_blk).astype(jnp.float32) + bias.astype(jnp.float32)
        s_win = jnp.where(mask, s_win, NEG_INF).reshape(b, H_B, n_cb, COL_QBLOCK, n_win)
        s_ctx = jnp.einsum('bnqhd,bchd->bhnqc', q_r, k_ctx).astype(jnp.float32)
        p = jax.nn.softmax(jnp.concatenate([s_win, s_ctx], axis=-1), axis=-1).astype(v.dtype)
        p_win = p[..., :n_win].reshape(b, H_B, n_cb, COL_QBLOCK, win_r, COL_KBLOCK)
        o = (jnp.einsum('bhnqrk,brnkhd->bnqhd', p_win, v_blk)
             + jnp.einsum('bhnqc,bchd->bnqhd', p[..., n_win:], v_ctx))
        return o.reshape(b, GRID_W, H_B, HD_B)

    out = lax.map(row_step, jnp.arange(rows, dtype=jnp.int32))
    return jnp.moveaxis(out, 0, 1).reshape(b, rows * GRID_W, H_B, HD_B)


def merge_branches(o_a, o_b, g_a, g_b, w_oa, w_ob, w_out):
    y_a = o_a.reshape(o_a.shape[:2] + (-1,)) @ w_oa
    y_b = o_b.reshape(o_b.shape[:2] + (-1,)) @ w_ob
    return (jax.nn.sigmoid(g_a) * y_a + jax.nn.sigmoid(g_b) * y_b) @ w_out


def swiglu(h, w_gu, w_down):
    gate, up = jnp.split(h @ w_gu, 2, axis=-1)
    return (jax.nn.silu(gate) * up) @ w_down


def setup_inputs(seed: int = 0) -> dict:
    key = jax.random.key(seed)
    ks = jax.random.split(key, 32)
    f32 = jnp.float32

    def nrm(k, shape, s=1.0):
        return s * jax.random.normal(k, shape, f32)

    def gain(k, shape):
        return 1.0 + 0.01 * jax.random.normal(k, shape, f32)

    return {
        'x_prompt': nrm(ks[0], (BATCH, SEQ, D_MODEL)),
        'x_sample': nrm(ks[1], (DEC_BATCH, DEC_SEQ, D_MODEL)),
        'cache_mla_ckv': nrm(ks[2], (DEC_BATCH, DEPTH, PAST_LEN, KV_LORA)),
        'cache_mla_krope': nrm(ks[3], (DEC_BATCH, DEPTH, PAST_LEN, ROPE_DIM)),
        'cache_na_k': nrm(ks[4], (DEC_BATCH, DEPTH, PAST_LEN, H_B, HD_B)),
        'cache_na_v': nrm(ks[5], (DEC_BATCH, DEPTH, PAST_LEN, H_B, HD_B)),
        'c': nrm(ks[6], (DEC_BATCH, D_MODEL)),
        'c_ctx': nrm(ks[7], (D_MODEL,)),
        'w_mod': nrm(ks[8], (DEPTH, D_MODEL, 6 * D_MODEL), 0.5 * D_MODEL ** -0.5),
        'b_mod': nrm(ks[9], (DEPTH, 6 * D_MODEL), 0.01),
        'norm1_g': gain(ks[10], (DEPTH, D_MODEL)),
        'w_in': nrm(ks[11], (DEPTH, D_MODEL, IN_COLS), D_MODEL ** -0.5),
        'q_norm_g': gain(ks[12], (DEPTH, Q_LORA)),
        'kv_norm_g': gain(ks[13], (DEPTH, KV_LORA)),
        'w_uq': nrm(ks[14], (DEPTH, Q_LORA, H_A * (NOPE_DIM + ROPE_DIM)), Q_LORA ** -0.5),
        'w_uk': nrm(ks[15], (DEPTH, KV_LORA, H_A * NOPE_DIM), KV_LORA ** -0.5),
        'w_uv': nrm(ks[16], (DEPTH, KV_LORA, H_A * V_DIM), KV_LORA ** -0.5),
        'rpb': nrm(ks[17], (DEPTH, H_B, 2 * WIN_R_MAX - 1, 2 * WIN_C - 1), 0.5),
        'w_oa': nrm(ks[18], (DEPTH, H_A * V_DIM, D_MODEL), (H_A * V_DIM) ** -0.5),
        'w_ob': nrm(ks[19], (DEPTH, NA_W, D_MODEL), NA_W ** -0.5),
        'w_out': nrm(ks[20], (DEPTH, D_MODEL, D_MODEL), D_MODEL ** -0.5),
        'norm2_g': gain(ks[21], (DEPTH, D_MODEL)),
        'w_gu': nrm(ks[22], (DEPTH, D_MODEL, 2 * D_FF), D_MODEL ** -0.5),
        'w_down': nrm(ks[23], (DEPTH, D_FF, D_MODEL), D_FF ** -0.5),
        'norm_f_g': gain(ks[24], (D_MODEL,)),
    }


def reference(x_prompt, x_sample, cache_mla_ckv, cache_mla_krope, cache_na_k, cache_na_v, c, c_ctx,
              w_mod, b_mod, norm1_g, w_in, q_norm_g, kv_norm_g, w_uq, w_uk, w_uv, rpb,
              w_oa, w_ob, w_out, norm2_g, w_gu, w_down, norm_f_g):
    s2 = x_sample.shape[1]
    rows = s2 // GRID_W
    cos, sin = axial_rope_tables(s2)
    xp, xs = x_prompt, x_sample
    ckv_list, krope_list, nak_list, nav_list = [], [], [], []
    for l in range(DEPTH):
        sh1, sc1, gt1, sh2, sc2, gt2 = adaln_params(c_ctx, w_mod[l], b_mod[l])
        h = modulate(rmsnorm(xp, norm1_g[l]), sh1, sc1)
        c_q, c_kv, k_rope, q_b, k_b, v_b, g_a, g_b = split_projection(h @ w_in[l])
        c_kv = rmsnorm(c_kv, kv_norm_g[l])
        q_a = mla_queries(c_q, q_norm_g[l], w_uq[l])
        k_a, v_a = mla_keys_values(c_kv, k_rope, w_uk[l], w_uv[l])
        o_a = dense_attention(q_a, k_a, v_a, MLA_SCALE)
        k_b, v_b = split_heads(k_b, H_B), split_heads(v_b, H_B)
        o_b = dense_attention(split_heads(q_b, H_B), k_b, v_b, NA_SCALE)
        xp = xp + gt1 * merge_branches(o_a, o_b, g_a, g_b, w_oa[l], w_ob[l], w_out[l])
        xp = xp + gt2 * swiglu(modulate(rmsnorm(xp, norm2_g[l]), sh2, sc2), w_gu[l], w_down[l])
        ckv_list.append(c_kv)
        krope_list.append(k_rope)
        nak_list.append(k_b)
        nav_list.append(v_b)

        sh1, sc1, gt1, sh2, sc2, gt2 = [m[:, None, :] for m in adaln_params(c, w_mod[l], b_mod[l])]
        h = modulate(rmsnorm(xs, norm1_g[l]), sh1, sc1)
        c_q, c_kv, k_rope, q_b, k_b, v_b, g_a, g_b = split_projection(h @ w_in[l])
        c_kv = rmsnorm(c_kv, kv_norm_g[l])
        q_a = mla_queries(c_q, q_norm_g[l], w_uq[l])
        q_a = jnp.concatenate([q_a[..., :NOPE_DIM], apply_rope(q_a[..., NOPE_DIM:], cos[:, None, :], sin[:, None, :])], axis=-1)
        k_a, v_a = mla_keys_values(c_kv, apply_rope(k_rope, cos, sin), w_uk[l], w_uv[l])
        k_ctx_a, v_ctx_a = mla_keys_values(cache_mla_ckv[:, l], cache_mla_krope[:, l], w_uk[l], w_uv[l])
        o_a = dense_attention(q_a, jnp.concatenate([k_a, k_ctx_a], axis=1), jnp.concatenate([v_a, v_ctx_a], axis=1), MLA_SCALE)
        o_b = neighborhood_attention(split_heads(q_b, H_B), split_heads(k_b, H_B), split_heads(v_b, H_B),
                                     cache_na_k[:, l], cache_na_v[:, l], rpb[l], rows)
        xs = xs + gt1 * merge_branches(o_a, o_b, g_a, g_b, w_oa[l], w_ob[l], w_out[l])
        xs = xs + gt2 * swiglu(modulate(rmsnorm(xs, norm2_g[l]), sh2, sc2), w_gu[l], w_down[l])

    y_prompt = rmsnorm(xp, norm_f_g)
    y_sample = rmsnorm(xs, norm_f_g)
    new_mla_ckv = jnp.stack(ckv_list, axis=1)
    new_mla_krope = jnp.stack(krope_list, axis=1)
    new_na_k = jnp.stack(nak_list, axis=1)
    new_na_v = jnp.stack(nav_list, axis=1)
    return (y_prompt, y_sample, new_mla_ckv, new_mla_krope, new_na_k, new_na_v)
```

```python
import numpy as np
from contextlib import ExitStack
import ml_dtypes
import concourse.bass as bass
import concourse.mybir as mybir
from concourse.bass_utils import run_bass_kernel_spmd

F32 = mybir.dt.float32
BF16 = mybir.dt.bfloat16
ALU = mybir.AluOpType
AF = mybir.ActivationFunctionType
AX = mybir.AxisListType

D = 2048
KC = 16
T = 256
SEQ = 256
GRID_W = 64
Q_LORA, KV_LORA, ROPE, NOPE, VD = 512, 256, 64, 128, 128
H = 8
HD = 128
DFF = 5632
NFF = 44
IN_COLS = 8000
O_CQ, O_CKV, O_KR, O_QB, O_KB, O_VB, O_GA, O_GB = 0, 512, 768, 832, 1856, 2880, 3904, 5952
EPS = 1e-6
MLA_SCALE = float((NOPE + ROPE) ** -0.5)
NA_SCALE = float(HD ** -0.5)
NEG = -30000.0
NK_MLA = 4096 + 256
NA_ROWS = 44
NK_NA = NA_ROWS * 64 + 256
SLAB = 4096
PREFETCH = 2
NSLOT = 3


class TT:
    __slots__ = ("name", "h", "recs", "psum")

    def __init__(self, name, h, psum=False):
        self.name = name
        self.h = h
        self.recs = {}
        self.psum = psum

    def __getitem__(self, k):
        return self.h[k]


class Op:
    __slots__ = ("eng", "fn", "deps", "is_dma", "dsem", "dval", "sig", "sigidx")

    def __init__(self, eng, fn, is_dma):
        self.eng = eng
        self.fn = fn
        self.deps = []
        self.is_dma = is_dma
        self.dsem = None
        self.dval = 0
        self.sig = False
        self.sigidx = None


ENGS = ("pe", "act", "dve", "pool", "sp")


class Sched:
    def __init__(self, nc):
        self.nc = nc
        self.ops = {e: [] for e in ENGS}
        self.dry = False
        self.dsem_counts = {}

    def _conf(self, t, key, write, op):
        recs = t.recs
        if key is None:
            keys = list(recs.keys())
        else:
            keys = [k for k in (key, None) if k in recs]
        for k in keys:
            w, rd, drd = recs[k]
            if w is not None:
                op.deps.append((w, "waw" if write else "raw"))
            if write:
                for r in rd.values():
                    op.deps.append((r, "war"))
                for r in drd:
                    op.deps.append((r, "war"))

    def _upd(self, t, key, write, op):
        recs = t.recs
        if write:
            if key is None:
                recs.clear()
            recs[key] = [op, {}, []]
        else:
            if key not in recs:
                recs[key] = [None, {}, []]
            if op.is_dma:
                recs[key][2].append(op)
            else:
                recs[key][1][op.eng] = op

    def op(self, eng, fn, reads=(), writes=(), dma=False, dsem=None):
        if self.dry:
            return None
        o = Op(eng, fn, dma)
        if any(t.psum for (t, k) in reads) or any(t.psum for (t, k) in writes):
            pw = []
            for (t, k) in list(reads) + list(writes):
                if t.psum and all(t is not q for (q, _) in pw):
                    pw.append((t, None))
            reads = [(t, k) for (t, k) in reads if not t.psum]
            writes = [(t, k) for (t, k) in writes if not t.psum] + pw
        for (t, k) in reads:
            self._conf(t, k, False, o)
        for (t, k) in writes:
            self._conf(t, k, True, o)
        for (t, k) in reads:
            self._upd(t, k, False, o)
        for (t, k) in writes:
            self._upd(t, k, True, o)
        if dma:
            c = self.dsem_counts.get(dsem, 0) + 16
            self.dsem_counts[dsem] = c
            o.dsem = dsem
            o.dval = c
        self.ops[eng].append(o)
        return o

    def emit(self, stack):
        nc = self.nc
        for e in ENGS:
            for o in self.ops[e]:
                nd = []
                seen = set()
                for (p, kind) in o.deps:
                    if id(p) in seen or p is o:
                        continue
                    if (not p.is_dma) and p.eng == o.eng and not o.is_dma:
                        if o.eng == "pe":
                            continue
                    seen.add(id(p))
                    nd.append(p)
                    if not p.is_dma:
                        p.sig = True
                o.deps = nd
        EPOCH = 16000
        nep = {}
        for e in ENGS:
            c = 0
            for o in self.ops[e]:
                if o.sig and not o.is_dma:
                    o.sigidx = (c // EPOCH, c % EPOCH + 1)
                    c += 1
            nep[e] = c // EPOCH + 1
        csem = {(e, i): stack.enter_context(nc.semaphore("cs_%s%d" % (e, i)))
                for e in ENGS for i in range(nep[e])}
        dsem = {k: stack.enter_context(nc.semaphore("ds_%d" % i))
                for i, k in enumerate(self.dsem_counts.keys())}
        block = stack.enter_context(nc.Block())
        sched = self

        def body(e):
            def f(eng):
                known = {}
                for o in sched.ops[e]:
                    need = {}
                    for p in o.deps:
                        if p.is_dma:
                            s, v = dsem[p.dsem], p.dval
                        else:
                            s, v = csem[(p.eng, p.sigidx[0])], p.sigidx[1]
                        key = id(s)
                        if known.get(key, 0) >= v:
                            continue
                        if key not in need or need[key][1] < v:
                            need[key] = (s, v)
                    for key, (s, v) in need.items():
                        eng.wait_ge(s, v)
                        known[key] = v
                    ins = o.fn(eng)
                    if o.is_dma:
                        ins.then_inc(dsem[o.dsem], 16)
                    elif o.sig:
                        ins.then_inc(csem[(e, o.sigidx[0])], 1)
                if e == "sp":
                    for k, c in sched.dsem_counts.items():
                        eng.wait_ge(dsem[k], c)
            return f

        block.tensor(body("pe"))
        block.scalar(body("act"))
        block.vector(body("dve"))
        block.gpsimd(body("pool"))
        block.sync(body("sp"))


def slab_catalogue():
    cat = {}

    def fm(wname, c0, ncols, nkc=16, width=256):
        return [(wname, 0, nkc, [(c, width)]) for c in range(c0, c0 + ncols, width)]

    def tm(wname, c0, ncols, K, width=512):
        out = []
        for c in range(c0, c0 + ncols, width):
            w = min(width, c0 + ncols - c)
            for k0 in range(0, K // 128, 8):
                out.append((wname, k0, min(8, K // 128 - k0), [(c, w)]))
        return out

    cat["in_cq"] = fm("w_in", O_CQ, 512)
    cat["in_qb"] = fm("w_in", O_QB, 1024)
    cat["in_ga"] = fm("w_in", O_GA, 2048)
    cat["in_gb"] = fm("w_in", O_GB, 2048)
    cat["in_kv"] = tm("w_in", O_CKV, 320, D, width=320)
    cat["in_kb"] = tm("w_in", O_KB, 1024, D)
    cat["in_vb"] = tm("w_in", O_VB, 1024, D)
    cat["oa"] = fm("w_oa", 0, 2048, nkc=8)
    cat["ob"] = fm("w_ob", 0, 2048, nkc=8)
    cat["out"] = tm("w_out", 0, 2048, D)
    cat["gu"] = [("w_gu", 0, 16, [(j * 128, 128), (DFF + j * 128, 128)]) for j in range(NFF)]
    cat["down"] = tm("w_down", 0, 2048, DFF)
    return cat


CAT = slab_catalogue()
SLAB_IDS = {}
_n = 0
for _k, _v in CAT.items():
    for _i in range(len(_v)):
        SLAB_IDS[(_k, _i)] = _n
        _n += 1
NSLAB = _n


def build_program(stage=99):
    nc = bass.Bass("TRN2", target_bir_lowering=False)
    S = Sched(nc)

    def din(name, shape, dt=F32):
        return nc.dram_tensor(name, list(shape), dt, kind="ExternalInput").ap()

    def dout(name, shape):
        return nc.dram_tensor(name, list(shape), F32, kind="ExternalOutput").ap()

    def dscr(name, shape, dt=BF16):
        return TT(name, nc.dram_tensor("d_" + name, list(shape), dt, kind="Internal").ap())

    xp = din("xp", [4, SEQ, D])
    xs_full = din("xs_full", [4096, D])
    xs_na = din("xs_na", [NA_ROWS * 64, D])
    xs_own = din("xs_own", [2048, D])
    c_ckv = din("c_ckv", [256, KV_LORA])
    c_kr = din("c_kr", [256, ROPE])
    c_nak = din("c_nak", [256, H * HD])
    c_nav = din("c_nav", [256, H * HD])
    W = {
        "w_in": din("w_in", [D, IN_COLS]), "w_oa": din("w_oa", [1024, D]), "w_ob": din("w_ob", [1024, D]),
        "w_out": din("w_out", [D, D]), "w_gu": din("w_gu", [D, 2 * DFF]), "w_down": din("w_down", [DFF, D]),
    }
    w_mod = din("w_mod", [D, 6 * D])
    w_uq = din("w_uq", [Q_LORA, H * 192])
    w_uk = din("w_uk", [KV_LORA, H * NOPE])
    w_uv = din("w_uv", [KV_LORA, H * VD])
    vecs = din("vecs", [128, 16 * 2 + 16 + 16 + 4 + 96])
    kvg_b_d = din("kvg_b", [128, KV_LORA])
    gf_b_d = din("gf_b", [128, D])
    ident_d = din("ident", [128, 128])
    rope_tm = din("rope_tm", [4096, 64])
    rope_fm = din("rope_fm", [64, 2, 2048])
    tb_d = din("tb", [128, 7 * H * 128])
    mk_d = din("mk", [2, 16 * 7 * 128], BF16)
    e2_d = din("e2", [128, 128], BF16)

    yp = dout("yp", [4, SEQ, D])
    ys = dout("ys", [2048, D])
    o_ckv = dout("o_ckv", [4, SEQ, KV_LORA])
    o_kr = dout("o_kr", [4, SEQ, ROPE])
    o_nak = dout("o_nak", [4, SEQ, H * HD])
    o_nav = dout("o_nav", [4, SEQ, H * HD])

    WB = dscr("WB", [NSLAB, 128, SLAB])
    CVT = [TT("cvt%d" % i, None) for i in range(8)]
    WMD = TT("wmod_done", None)
    KTm_p = dscr("KTm_p", [4, H, 128, SEQ]); KR_p = dscr("KR_p", [4, 128, SEQ]); Vm_p = dscr("Vm_p", [4, H, 128, SEQ // 128, VD])
    KTn_p = dscr("KTn_p", [4, H, 128, SEQ]); Vn_p = dscr("Vn_p", [4, H, 128, SEQ // 128, HD])
    KTm_s = dscr("KTm_s", [H, 128, NK_MLA]); KR_s = dscr("KR_s", [128, NK_MLA]); Vm_s = dscr("Vm_s", [H, 128, NK_MLA // 128, VD])
    KTn_s = dscr("KTn_s", [H, 128, NK_NA]); Vn_s = dscr("Vn_s", [H, 128, NK_NA // 128, HD])

    with ExitStack() as st:
        def sb(name, shape, dt):
            return TT(name, st.enter_context(nc.sbuf_tensor("s_" + name, list(shape), dt)))

        def ps(name, shape, dt):
            return TT(name, st.enter_context(nc.psum_tensor("p_" + name, list(shape), dt)), psum=True)

        ident_f = sb("ident_f", [128, 128], F32)
        ident_b = sb("ident_b", [128, 128], BF16)
        ones_f = sb("ones_f", [128, 128], F32)
        ones_b = sb("ones_b", [128, 128], BF16)
        e2 = sb("e2", [128, 128], BF16)
        vec = sb("vec", [128, 164], F32)
        silu_c = sb("silu_c", [128, 16, 2], F32)
        mrow = sb("mrow", [2, 512], F32)
        modfm = sb("modfm", [128, 96, 2], F32)
        AB = sb("AB", [128, 2, 4, 16], F32)
        gtb = [sb("gt1b", [128, D], F32), sb("gt2b", [128, D], F32)]
        gfb = sb("gfb", [128, D], F32)
        kvgb = sb("kvgb", [128, KV_LORA], F32)
        wuq = sb("wuq", [128, 4, H * 192 + 64], BF16)
        wuqr = sb("wuqr", [128, 4, H * 64 + 64], BF16)
        wuk = sb("wuk", [128, 2, H * NOPE], BF16)
        wuv = sb("wuv", [128, 2, H * VD], BF16)
        tbt = sb("tbt", [128, 7, H, 128], BF16)
        kmax = {k: sb("kmax_" + k, [128, H], F32) for k in ("mla", "na")}
        kmaxr = sb("kmaxr", [128, 1], F32)
        xt = sb("xt", [128, 2, D], F32)
        xn = sb("xn", [128, D], BF16)
        xn2 = sb("xn2", [128, D], BF16)
        hT = sb("hT", [128, KC, T], BF16)
        cqT = sb("cqT", [128, 4, T], BF16)
        U = sb("U", [128, NFF * T], BF16)
        slabs = [sb("slab%d" % i, [128, SLAB], BF16) for i in range(NSLOT)]
        ystage = sb("ystage", [128, D], F32)
        tmpf = sb("tmpf", [128, 512], F32)
        tmpg = sb("tmpg", [128, 512], F32)
        small = sb("small", [128, 64], F32)
        epst = sb("epst", [128, 1], F32)
        ckvst = sb("ckvst", [128, 2, 320], F32)
        ktst = sb("ktst", [128, H, T], BF16)
        krst = sb("krst", [128, T], BF16)
        tmk = sb("tmk", [128, 384], BF16)
        ckvT = sb("ckvT", [128, 2, T], BF16)
        vst = sb("vst", [128, 2, H * VD], BF16)
        sq = sb("sq", [128, T], BF16)
        sqr = sb("sqr", [128, T], BF16)
        ropt = sb("ropt", [128, 2, 64], F32)
        ropf = sb("ropf", [64, 2, T], F32)
        ropw = sb("ropw", [128, 6, 32], F32)
        mkt = sb("mkt", [128, 2 * 7 * 128], BF16)
        qn = [sb("qn%d" % i, [128, T], BF16) for i in range(2)]
        qr = [sb("qr%d" % i, [128, T], BF16) for i in range(2)]
        bias_t = [sb("bias%d" % i, [128, 2], F32) for i in range(2)]
        ktb = [sb("ktb%d" % i, [128, 1280], BF16) for i in range(2)]
        krr = sb("krr", [128, NK_MLA], BF16)
        vb = [sb("vb%d" % i, [128, 10, 128], BF16) for i in range(2)]
        pt = [sb("pt%d" % i, [128, 512], BF16) for i in range(3)]
        rinv = sb("rinv", [128, T], F32)
        PB = [ps("pb%d" % i, [128, 512], F32) for i in range(6)]
        PBT = [ps("pbt%d" % i, [128, 1024], BF16) for i in range(2)]

        ACC = [(PB[4], PB[5]), (TT("pbt0f", PBT[0][:, :].bitcast(F32), psum=True), TT("pbt1f", PBT[1][:, :].bitcast(F32), psum=True))]
        ACC[1][0].recs = PBT[0].recs
        ACC[1][1].recs = PBT[1].recs
        NB = [PBT[0], PBT[1], TT("pb4b", PB[4][:, :].bitcast(BF16), psum=True), TT("pb5b", PB[5][:, :].bitcast(BF16), psum=True)]
        NB[2].recs = PB[4].recs
        NB[3].recs = PB[5].recs
        print("sbuf bytes remaining", nc.sbuf_bytes_remaining, flush=True)
        oaT = lambda h: U[:, h * T:(h + 1) * T]
        obT = lambda h: U[:, (H + h) * T:(H + h + 1) * T]
        mgT = lambda kc: U[:, (16 + kc) * T:(16 + kc + 1) * T]
        actT = lambda j: U[:, j * T:(j + 1) * T]

        rot = {"pb": 0, "pbt": 0, "pt": 0}
        store_q = ["sp"]

        def nxt(kind, n):
            i = rot[kind]
            rot[kind] = (i + 1) % n
            return i

        def dma(q, out, in_, reads=(), writes=(), dsem=None, **kw):
            S.op(q, lambda e: e.dma_start(out=out, in_=in_, **kw), reads=reads, writes=writes, dma=True, dsem=dsem)

        def mm(out, lhsT, rhs, start, stop, reads, writes):
            S.op("pe", lambda e: e.matmul(out, lhsT=lhsT, rhs=rhs, start=start, stop=stop), reads=reads, writes=writes)

        def tr(out, in_, idn, reads, writes):
            S.op("pe", lambda e: e.transpose(out=out, in_=in_, identity=idn), reads=reads, writes=writes)

        def act(out, in_, func, reads, writes, **kw):
            S.op("act", lambda e: e.activation(out=out, in_=in_, func=func, **kw), reads=reads, writes=writes)

        def tt(out, in0, in1, op, reads, writes, eng="dve"):
            S.op(eng, lambda e: e.tensor_tensor(out=out, in0=in0, in1=in1, op=op), reads=reads, writes=writes)

        def ts(out, in0, s1, s2, op0, op1, reads, writes, eng="dve"):
            if s2 is None:
                S.op(eng, lambda e: e.tensor_scalar(out=out, in0=in0, scalar1=s1, scalar2=None, op0=op0),
                     reads=reads, writes=writes)
            else:
                S.op(eng, lambda e: e.tensor_scalar(out=out, in0=in0, scalar1=s1, scalar2=s2, op0=op0, op1=op1),
                     reads=reads, writes=writes)

        def cp(out, in_, reads, writes, eng="dve"):
            S.op(eng, lambda e: e.tensor_copy(out=out, in_=in_), reads=reads, writes=writes)

        stream = {"list": [], "pos": 0, "issued": 0}

        def slab_view(spec):
            _, k0, nkc, cols = spec
            wdt = sum(c[1] for c in cols)
            return nkc, wdt

        def issue_load(i):
            name, idx = stream["list"][i]
            spec = CAT[name][idx]
            nkc, wdt = slab_view(spec)
            sid = SLAB_IDS[(name, idx)]
            slot = slabs[i % NSLOT]
            n = nkc * wdt
            dma("sp", slot[:, 0:n], WB[sid, :, 0:n], reads=[(WB, sid)], writes=[(slot, None)], dsem="slab%d" % (i % NSLOT))

        def get_slab(name, idx):
            i = stream["pos"]
            stream["pos"] += 1
            if S.dry:
                stream["list"].append((name, idx))
                return None, None
            assert stream["list"][i] == (name, idx), (i, stream["list"][i], name, idx)
            while stream["issued"] <= min(i + PREFETCH, len(stream["list"]) - 1):
                issue_load(stream["issued"])
                stream["issued"] += 1
            slot = slabs[i % NSLOT]
            nkc, wdt = slab_view(CAT[name][idx])
            return slot, slot[:, 0:nkc * wdt].rearrange("p (k c) -> p k c", c=wdt)

        def prologue():
            dma("sp", ident_f[:], ident_d, writes=[(ident_f, None)], dsem="c_ident")
            dma("sp", vec[:], vecs, writes=[(vec, None)], dsem="c_vec")
            dma("sp", kvgb[:], kvg_b_d, writes=[(kvgb, None)], dsem="c_kvgb")
            dma("sp", gfb[:], gf_b_d, writes=[(gfb, None)], dsem="c_gfb")
            dma("sp", e2[:], e2_d, writes=[(e2, None)], dsem="c_e2")
            cp(ident_b[:], ident_f[:], [(ident_f, None)], [(ident_b, None)])
            S.op("dve", lambda e: e.memset(ones_f[:], 1.0), writes=[(ones_f, None)])
            S.op("dve", lambda e: e.memset(epst[:], EPS), writes=[(epst, None)])
            S.op("dve", lambda e: e.memset(ones_b[:], 1.0), writes=[(ones_b, None)])
            for k in list(kmax.values()) + [kmaxr]:
                S.op("dve", lambda e, k=k: e.memset(k[:], 0.0), writes=[(k, None)])
            for z_ in (wuq, wuqr, tmk, mkt, qr[0], qr[1], krr):
                S.op("pool", lambda e, z_=z_: e.memset(z_[:], 0.0), writes=[(z_, None)])
            dma("pool", wuk[:], w_uk.rearrange("(k p) m -> p k m", p=128), writes=[(wuk, None)], dsem="c_wuk")
            dma("pool", wuv[:], w_uv.rearrange("(k p) m -> p k m", p=128), writes=[(wuv, None)], dsem="c_wuv")
            dma("pool", tbt[:].rearrange("p a h k -> p a (h k)"), tb_d.rearrange("p (a x) -> p a x", a=7),
                writes=[(tbt, None)], dsem="c_tbt")
            for kc in range(4):
                stg = slabs[kc % NSLOT]
                stf = stg[:, :].bitcast(F32)
                dma("sp", stf[:, 0:1536], w_uq[kc * 128:(kc + 1) * 128, :], writes=[(stg, None)], dsem="wq%d" % (kc % NSLOT))
                src = stf[:, 0:1536].rearrange("p (h c) -> p h c", c=192)
                g = vec[:, 64 + kc:65 + kc]
                wq3 = wuq[:, kc, 0:H * 192].rearrange("p (h c) -> p h c", c=192)
                wr3 = wuqr[:, kc, 0:H * 64].rearrange("p (h c) -> p h c", c=64)
                ts(wq3, src, g, None, ALU.mult, None, [(stg, None), (vec, None), (wuq, None)], [(wuq, kc)])
                ts(wr3[:, :, 0:32], src[:, :, 160:192], g, -1.0, ALU.mult, ALU.mult, [(stg, None), (vec, None), (wuqr, None)], [(wuqr, (kc, 0))])
                ts(wr3[:, :, 32:64], src[:, :, 128:160], g, None, ALU.mult, None, [(stg, None), (vec, None), (wuqr, None)], [(wuqr, (kc, 1))])
            act(silu_c[:].rearrange("p k c -> p (k c)"), vec[:, 0:32], AF.Silu, [(vec, None)], [(silu_c, None)])
            li = 0
            for cb in range(24):
                pb = PB[cb % 2]
                for ks in range(4):
                    stg = slabs[li % NSLOT]
                    stf = stg[:, :].bitcast(F32).rearrange("p (k c) -> p k c", c=512)
                    dma("sp", stf, w_mod[ks * 512:(ks + 1) * 512, cb * 512:(cb + 1) * 512].rearrange("(k p) c -> p k c", p=128),
                        writes=[(stg, None)] + ([(WMD, None)] if li == 95 else []), dsem="wq%d" % (li % NSLOT))
                    li += 1
                    for k in range(4):
                        kc = ks * 4 + k
                        mm(pb[0:2, :], silu_c[:, kc, :], stf[:, k, :], kc == 0, kc == KC - 1,
                           [(stg, None), (silu_c, None)], [(pb, None)])
                cp(mrow[:], pb[0:2, :], [(pb, None)], [(mrow, None)])
                pT = PB[2 + cb % 2]
                for j in range(4):
                    tr(pT[:, 2 * j:2 * j + 2], mrow[:, j * 128:(j + 1) * 128], ident_f[0:2, 0:2],
                       [(mrow, None), (ident_f, None)], [(pT, None)])
                for c in range(2):
                    tt(modfm[:, 4 * cb:4 * cb + 4, c], pT[:, 0:8].rearrange("p (j c) -> p j c", c=2)[:, :, c],
                       vec[:, 68 + 4 * cb:72 + 4 * cb], ALU.add, [(pT, None), (vec, None)], [(modfm, (cb, c))])
            for c in range(2):
                for j, (ps_, pg) in enumerate(((1, 32), (4, 48))):
                    S.op("dve", lambda e, c=c, j=j, ps_=ps_, pg=pg: e.scalar_tensor_tensor(
                        out=AB[:, c, 2 * j, :], in0=modfm[:, ps_ * 16:(ps_ + 1) * 16, c], scalar=1.0,
                        in1=vec[:, pg:pg + 16], op0=ALU.add, op1=ALU.mult),
                        reads=[(modfm, None), (vec, None)], writes=[(AB, (c, 2 * j))])
                    sh = ps_ - 1
                    cp(AB[:, c, 2 * j + 1, :], modfm[:, sh * 16:(sh + 1) * 16, c], [(modfm, None)], [(AB, (c, 2 * j + 1))])

        def convert_weights():
            order = [(n, i) for n in ("in_kv", "in_kb", "in_vb", "in_cq", "in_qb") for i in range(len(CAT[n]))]
            for r in range(8):
                order += [("in_ga", r), ("oa", r), ("in_gb", r), ("ob", r)]
            order += [(n, i) for n in ("out", "gu", "down") for i in range(len(CAT[n]))]
            assert sorted(order) == sorted(SLAB_IDS.keys())
            for ci, (name, idx) in enumerate(order):
                sid = SLAB_IDS[(name, idx)]
                wname, k0, nkc, cols = CAT[name][idx]
                wdt = sum(c[1] for c in cols)
                dst = WB[sid, :, 0:nkc * wdt].rearrange("p (k c) -> p k c", c=wdt)
                off = 0
                for (c0, w) in cols:
                    src = W[wname][k0 * 128:(k0 + nkc) * 128, c0:c0 + w].rearrange("(k p) c -> p k c", p=128)
                    ch = (4 + ci % 4) if ci < 10 else (ci % 4)
                    dma("pool", dst[:, :, off:off + w], src, reads=([(WMD, None)] if ci == 10 else []),
                        writes=[(WB, sid), (CVT[ch], None)], dsem="cv%d" % ch)
                    off += w

        def build_gates(c):
            for gi, prm in enumerate((2, 5)):
                for blk in range(4):
                    pb = PB[nxt("pb", 4)]
                    for j in range(4):
                        m = blk * 4 + j
                        ts(tmpf[:, j * 128:(j + 1) * 128], ident_f[:], modfm[:, prm * 16 + m, c:c + 1], None, ALU.mult, None,
                           [(ident_f, None), (modfm, None)], [(tmpf, j)])
                        mm(pb[:, j * 128:(j + 1) * 128], ones_f[:], tmpf[:, j * 128:(j + 1) * 128], True, True,
                           [(ones_f, None), (tmpf, j)], [(pb, None)])
                    cp(gtb[gi][:, blk * 512:(blk + 1) * 512], pb[:], [(pb, None)], [(gtb[gi], blk)])

        def load_x(src_ap):
            dma("sp", xt[:], src_ap.rearrange("(t p) d -> p t d", p=128), writes=[(xt, None)], dsem="xt")

        def rstd_of(ss_ap, n, out_ap, rd, wr):
            act(out_ap, ss_ap, AF.Sqrt, list(rd) + [(epst, None)], wr, scale=1.0 / n, bias=epst[:, 0:1])
            S.op("dve", lambda e: e.reciprocal(out=out_ap, in_=out_ap), reads=wr, writes=wr)

        def norm_mod(c, which):
            xns = (xn, xn2)
            act(xn[:], xt[:, 0, :], AF.Square, [(xt, None)], [(xn, None), (small, 0)], accum_out=small[:, 0:1])
            S.op("dve", lambda e: e.scalar_tensor_tensor(out=xn2[:], in0=xt[:, 1, :], scalar=1.0, in1=xt[:, 1, :],
                                                         op0=ALU.mult, op1=ALU.mult, accum_out=small[:, 2:3]),
                 reads=[(xt, None)], writes=[(xn2, None), (small, 2)])
            rstd_of(small[:, 0:1], D, small[:, 1:2], [(small, 0)], [(small, 1)])
            rstd_of(small[:, 2:3], D, small[:, 3:4], [(small, 2)], [(small, 3)])
            ts(xn[:], xt[:, 0, :], small[:, 1:2], None, ALU.mult, None, [(xt, None), (small, 1)], [(xn, None)])
            act(xn2[:], xt[:, 1, :], AF.Copy, [(xt, None), (small, 3)], [(xn2, None)], scale=small[:, 3:4])
            for t_ in range(2):
                for half in range(2):
                    pr = nxt("pbt", 2)
                    bE, bO = NB[2 * pr], NB[2 * pr + 1]
                    for k in range(8):
                        kc = half * 8 + k
                        bk = bE if k % 2 == 0 else bO
                        sl = k // 2
                        tr(bk[:, sl * 128:(sl + 1) * 128], xns[t_][:, kc * 128:(kc + 1) * 128], ident_b[:],
                           [(xns[t_], None), (ident_b, None)], [(bk, sl)])
                    for k in range(8):
                        kc = half * 8 + k
                        sl = k // 2
                        if k % 2 == 0:
                            act(hT[:, kc, t_ * 128:(t_ + 1) * 128], bE[:, sl * 128:(sl + 1) * 128], AF.Identity,
                                [(bE, sl), (AB, None)], [(hT, (kc, t_))],
                                scale=AB[:, c, 2 * which, kc:kc + 1], bias=AB[:, c, 2 * which + 1, kc:kc + 1])
                        else:
                            ts(hT[:, kc, t_ * 128:(t_ + 1) * 128], bO[:, sl * 128:(sl + 1) * 128],
                               AB[:, c, 2 * which, kc:kc + 1], AB[:, c, 2 * which + 1, kc:kc + 1], ALU.mult, ALU.add,
                               [(bO, sl), (AB, None)], [(hT, (kc, t_))])

        def tm_matmul(name, nblk, lhs_of, nkc_total, evac):
            per_blk = len(CAT[name]) // nblk
            for blk in range(nblk):
                pbs = [PB[nxt("pb", 4)] for _ in range(2)]
                wdt = None
                kk = 0
                for si in range(per_blk):
                    slot, v = get_slab(name, blk * per_blk + si)
                    if S.dry:
                        continue
                    nkc, wdt = slab_view(CAT[name][blk * per_blk + si])
                    for ki in range(nkc):
                        for t_ in range(2):
                            lt, lap = lhs_of(kk, t_)
                            mm(pbs[t_][:, 0:wdt], lap, v[:, ki, :], kk == 0, kk == nkc_total - 1,
                               [(slot, None), lt], [(pbs[t_], None)])
                        kk += 1
                if not S.dry:
                    evac(blk, pbs, wdt)

        def kmax_update_all(kind, with_rope):
            sqa = U[:, 0:H * T]
            tt(sqa, ktst[:].rearrange("p h t -> p (h t)"), ktst[:].rearrange("p h t -> p (h t)"), ALU.mult,
               [(ktst, None)], [(U, "kvsq")])
            for pr in range(4):
                pb = PB[nxt("pb", 4)]
                mm(pb[:], ones_b[:], sqa[:, pr * 512:(pr + 1) * 512], True, True, [(ones_b, None), (U, "kvsq")], [(pb, None)])
                S.op("dve", lambda e, pb=pb, pr=pr: e.reduce_max(out=small[:, 12 + 2 * pr:14 + 2 * pr],
                                                                 in_=pb[:].rearrange("p (h t) -> p h t", t=T), axis=AX.X),
                     reads=[(pb, None)], writes=[(small, 12 + pr)])
            tt(kmax[kind][:], kmax[kind][:], small[:, 12:20], ALU.max,
               [(small, 12), (small, 13), (small, 14), (small, 15), (kmax[kind], None)], [(kmax[kind], None)])
            if with_rope:
                tt(sqr[:], krst[:], krst[:], ALU.mult, [(krst, None)], [(sqr, None)])
                pb = PB[nxt("pb", 4)]
                mm(pb[:, 0:T], ones_b[:], sqr[:], True, True, [(ones_b, None), (sqr, None)], [(pb, None)])
                S.op("dve", lambda e, pb=pb: e.reduce_max(out=small[:, 8:9], in_=pb[:, 0:T], axis=AX.X),
                     reads=[(pb, None)], writes=[(small, 8)])
                tt(kmaxr[:], kmaxr[:], small[:, 8:9], ALU.max, [(small, 8), (kmaxr, None)], [(kmaxr, None)])

        def kv_mla_from_tm(src_f32, src_dep, nt_tok0, rope, out_idx, KTd, KRd, Vd, koff, seq_idx):
            for t_ in range(2):
                cp(tmk[:, 0:320], src_f32(t_), [src_dep(t_)], [(tmk, 0)])
                pbt = PBT[nxt("pbt", 2)]
                for k in range(3):
                    tr(pbt[:, k * 128:(k + 1) * 128], tmk[:, k * 128:(k + 1) * 128], ident_b[:],
                       [(tmk, None), (ident_b, None)], [(pbt, k)])
                cp(ckvT[:, :, t_ * 128:(t_ + 1) * 128], pbt[:, 0:256].rearrange("p (k t) -> p k t", t=128),
                   [(pbt, None)], [(ckvT, t_)])
                cp(krst[:, t_ * 128:(t_ + 1) * 128], pbt[:, 256:384], [(pbt, None)], [(krst, t_)])
            kd = KRd[seq_idx] if seq_idx is not None else KRd[:, :]
            dma(store_q[0], kd[:, koff:koff + T] if seq_idx is None else kd, krst[:], reads=[(krst, None)],
                writes=[(KRd, (seq_idx, koff))], dsem="krst")
            for h in range(H):
                pb = PB[nxt("pb", 4)]
                for k in range(2):
                    mm(pb[:, 0:T], wuk[:, k, h * 128:(h + 1) * 128], ckvT[:, k, :], k == 0, k == 1,
                       [(wuk, None), (ckvT, None)], [(pb, None)])
                if h % 2 == 0:
                    act(ktst[:, h, :], pb[:, 0:T], AF.Copy, [(pb, None)], [(ktst, h)])
                else:
                    cp(ktst[:, h, :], pb[:, 0:T], [(pb, None)], [(ktst, h)])
            kmax_update_all("mla", True)
            if seq_idx is None:
                dst = KTd[:, :, koff:koff + T].rearrange("h p k -> p h k")
            else:
                dst = KTd[seq_idx].rearrange("h p k -> p h k")
            dma(store_q[0], dst, ktst[:], reads=[(ktst, None)], writes=[(KTd, (seq_idx, koff))], dsem="ktst")
            for t_ in range(2):
                for cb in range(2):
                    pb = PB[nxt("pb", 4)]
                    for k in range(2):
                        mm(pb[:], ckvT[:, k, t_ * 128:(t_ + 1) * 128], wuv[:, k, cb * 512:(cb + 1) * 512], k == 0, k == 1,
                           [(ckvT, None), (wuv, None)], [(pb, None)])
                    act(vst[:, t_, cb * 512:(cb + 1) * 512], pb[:], AF.Copy, [(pb, None)], [(vst, (t_, cb))])
            v4 = Vd[seq_idx] if seq_idx is not None else Vd[:, :, :, :]
            for t_ in range(2):
                dma(store_q[0], v4[:, :, koff // 128 + t_, :].rearrange("h p d -> p h d"),
                    vst[:, t_, :].rearrange("p (h d) -> p h d", d=128), reads=[(vst, None)],
                    writes=[(Vd, (seq_idx, koff, t_))], dsem="vst")

        def kv_na_from_tm(k_bf_of, k_dep, v_bf_tile_written, KTd, Vd, koff, seq_idx):
            for t_ in range(2):
                pbt = PBT[nxt("pbt", 2)]
                for h in range(H):
                    tr(pbt[:, h * 128:(h + 1) * 128], k_bf_of(t_)[:, h * 128:(h + 1) * 128], ident_b[:],
                       list(k_dep(t_)) + [(ident_b, None)], [(pbt, h)])
                cp(ktst[:, :, t_ * 128:(t_ + 1) * 128], pbt[:].rearrange("p (h t) -> p h t", t=128),
                   [(pbt, None)], [(ktst, ("t", t_))])
            kmax_update_all("na", False)
            if seq_idx is None:
                dst = KTd[:, :, koff:koff + T].rearrange("h p k -> p h k")
            else:
                dst = KTd[seq_idx].rearrange("h p k -> p h k")
            dma(store_q[0], dst, ktst[:], reads=[(ktst, None)], writes=[(KTd, (seq_idx, koff))], dsem="ktst")
            v4 = Vd[seq_idx] if seq_idx is not None else Vd[:, :, :, :]
            for t_ in range(2):
                dma(store_q[0], v4[:, :, koff // 128 + t_, :].rearrange("h p d -> p h d"),
                    vst[:, t_, :].rearrange("p (h d) -> p h d", d=128), reads=[(vst, None)],
                    writes=[(Vd, (seq_idx, koff, t_))], dsem="vst")

        def kv_latent(c, rope_tok0, prompt_idx, koff):
            def lhs(kk, t_):
                return (hT, (kk, t_)), hT[:, kk, t_ * 128:(t_ + 1) * 128]

            def evac(blk, pbs, wdt):
                for t_ in range(2):
                    pb = pbs[t_]
                    act(tmpf[:, 0:256], pb[:, 0:256], AF.Square, [(pb, None)], [(tmpf, None), (small, 2)],
                        accum_out=small[:, 2:3])
                    rstd_of(small[:, 2:3], KV_LORA, small[:, 3:4], [(small, 2)], [(small, 3)])
                    act(ckvst[:, t_, 0:256], pb[:, 0:256], AF.Copy, [(pb, None), (small, 3)], [(ckvst, t_)], scale=small[:, 3:4])
                    tt(ckvst[:, t_, 0:256], ckvst[:, t_, 0:256], kvgb[:], ALU.mult, [(ckvst, t_), (kvgb, None)], [(ckvst, t_)])
                    if rope_tok0 is None:
                        act(ckvst[:, t_, 256:320], pb[:, 256:320], AF.Copy, [(pb, None)], [(ckvst, t_)])
                    else:
                        x1, x2 = pb[:, 256:288], pb[:, 288:320]
                        co, si = ropt[:, t_, 0:32], ropt[:, t_, 32:64]
                        rd = [(pb, None), (ropt, None)]
                        tt(ropw[:, 0, :], x1, co, ALU.mult, rd, [(ropw, 0)])
                        tt(ropw[:, 1, :], x2, si, ALU.mult, rd, [(ropw, 1)])
                        tt(ropw[:, 2, :], x2, co, ALU.mult, rd, [(ropw, 2)])
                        tt(ropw[:, 3, :], x1, si, ALU.mult, rd, [(ropw, 3)])
                        tt(ckvst[:, t_, 256:288], ropw[:, 0, :], ropw[:, 1, :], ALU.subtract, [(ropw, 0), (ropw, 1)], [(ckvst, t_)])
                        tt(ckvst[:, t_, 288:320], ropw[:, 2, :], ropw[:, 3, :], ALU.add, [(ropw, 2), (ropw, 3)], [(ckvst, t_)])

            if rope_tok0 is not None:
                dma("sp", ropt[:], rope_tm[rope_tok0:rope_tok0 + T, :].rearrange("(t p) c -> p t c", p=128),
                    writes=[(ropt, None)], dsem="ropt")
            tm_matmul("in_kv", 1, lhs, KC, evac)
            if S.dry:
                return
            import os
            if int(os.environ.get("KSUB", "9")) < 2:
                return
            if prompt_idx is not None:
                dma(store_q[0], o_ckv[prompt_idx].rearrange("(t p) c -> p t c", p=128), ckvst[:, :, 0:256],
                    reads=[(ckvst, None)], dsem="ckvst")
                dma(store_q[0], o_kr[prompt_idx].rearrange("(t p) c -> p t c", p=128), ckvst[:, :, 256:320],
                    reads=[(ckvst, None)], dsem="ckvst")
                kv_mla_from_tm(lambda t_: ckvst[:, t_, :], lambda t_: (ckvst, t_), None, False, None,
                               KTm_p, KR_p, Vm_p, 0, prompt_idx)
            else:
                kv_mla_from_tm(lambda t_: ckvst[:, t_, :], lambda t_: (ckvst, t_), None, True, None,
                               KTm_s, KR_s, Vm_s, koff, None)

        kst = lambda t_: U[:, 2048 + t_ * 1024:2048 + (t_ + 1) * 1024]
        kst_dep = lambda t_: [(U, ("kst", t_, 0)), (U, ("kst", t_, 1))]

        def kv_na(prompt_idx, koff):
            def lhs(kk, t_):
                return (hT, (kk, t_)), hT[:, kk, t_ * 128:(t_ + 1) * 128]

            def evac_k(blk, pbs, wdt):
                for t_ in range(2):
                    if prompt_idx is not None:
                        act(ystage[:, blk * 512:(blk + 1) * 512], pbs[t_][:], AF.Copy, [(pbs[t_], None)], [(ystage, blk)])
                        dma(store_q[0], o_nak[prompt_idx, t_ * 128:(t_ + 1) * 128, blk * 512:(blk + 1) * 512],
                            ystage[:, blk * 512:(blk + 1) * 512], reads=[(ystage, blk)], dsem="ysk%d" % blk)
                    cp(kst(t_)[:, blk * 512:(blk + 1) * 512], pbs[t_][:], [(pbs[t_], None)], [(U, ("kst", t_, blk))])

            def evac_v(blk, pbs, wdt):
                for t_ in range(2):
                    if prompt_idx is not None:
                        act(ystage[:, 1024 + blk * 512:1024 + (blk + 1) * 512], pbs[t_][:], AF.Copy,
                            [(pbs[t_], None)], [(ystage, 2 + blk)])
                        dma(store_q[0], o_nav[prompt_idx, t_ * 128:(t_ + 1) * 128, blk * 512:(blk + 1) * 512],
                            ystage[:, 1024 + blk * 512:1024 + (blk + 1) * 512], reads=[(ystage, 2 + blk)], dsem="ysv%d" % blk)
                    cp(vst[:, t_, blk * 512:(blk + 1) * 512], pbs[t_][:], [(pbs[t_], None)], [(vst, (t_, blk))])

            tm_matmul("in_kb", 2, lhs, KC, evac_k)
            tm_matmul("in_vb", 2, lhs, KC, evac_v)
            if S.dry:
                return
            if prompt_idx is not None:
                kv_na_from_tm(kst, kst_dep, None, KTn_p, Vn_p, 0, prompt_idx)
            else:
                kv_na_from_tm(kst, kst_dep, None, KTn_s, Vn_s, koff, None)

        def kv_ctx():
            dma("sp", ckvst[:, :, 0:256], c_ckv.rearrange("(t p) c -> p t c", p=128), writes=[(ckvst, None)], dsem="ckvst_l")
            dma("sp", ckvst[:, :, 256:320], c_kr.rearrange("(t p) c -> p t c", p=128), writes=[(ckvst, None)], dsem="ckvst_l")
            kv_mla_from_tm(lambda t_: ckvst[:, t_, :], lambda t_: (ckvst, None), None, False, None,
                           KTm_s, KR_s, Vm_s, 4096, None)
            for t_ in range(2):
                dma("sp", ystage[:, 0:1024], c_nak[t_ * 128:(t_ + 1) * 128, :], writes=[(ystage, None)], dsem="ystage_l")
                cp(kst(t_), ystage[:, 0:1024], [(ystage, None)], kst_dep(t_))
                dma("sp", ystage[:, 1024:2048], c_nav[t_ * 128:(t_ + 1) * 128, :], writes=[(ystage, None)], dsem="ystage_l")
                cp(vst[:, t_, :], ystage[:, 1024:2048], [(ystage, None)], [(vst, (t_, None))])
            kv_na_from_tm(kst, kst_dep, None, KTn_s, Vn_s, NA_ROWS * 64, None)

        def q_bias(kind, h, b, scale):
            pb = PB[nxt("pb", 4)]
            tt(sq[:], qn[b][:], qn[b][:], ALU.mult, [(qn[b], None)], [(sq, None)])
            mm(pb[:, 0:T], ones_b[:], sq[:], True, kind != "mla", [(ones_b, None), (sq, None)], [(pb, None)])
            if kind == "mla":
                tt(sqr[:], qr[b][:], qr[b][:], ALU.mult, [(qr[b], None)], [(sqr, None)])
                mm(pb[:, 0:T], ones_b[:], sqr[:], False, True, [(ones_b, None), (sqr, None)], [(pb, None)])
            S.op("dve", lambda e: e.reduce_max(out=small[:, 9:10], in_=pb[:, 0:T], axis=AX.X),
                 reads=[(pb, None)], writes=[(small, 9)])
            if kind == "mla":
                ts(small[:, 10:11], kmax[kind][:, h:h + 1], kmaxr[:, 0:1], scale * scale, ALU.add, ALU.mult,
                   [(kmax[kind], None), (kmaxr, None)], [(small, 10)])
            else:
                ts(small[:, 10:11], kmax[kind][:, h:h + 1], scale * scale, None, ALU.mult, None,
                   [(kmax[kind], None)], [(small, 10)])
            ts(bias_t[b][:, 0:1], small[:, 10:11], small[:, 9:10], -0.51 / scale, ALU.add, ALU.mult,
               [(small, 10), (small, 9)], [(bias_t[b], None)])

        def attn_block(b, nq0, nq, chunks, kt_parts, v_of, acc, first, last, out_ap, out_dep, extra=None):
            po, psu, c0 = acc
            n = len(chunks)
            bs = 512 // nq
            batches = [chunks[i:i + bs] for i in range(0, n, bs)]

            def qk(batch):
                pb = PB[nxt("pb", 4)]
                for bi_, c in enumerate(batch):
                    ex = extra(c) if extra else []
                    np_ = len(kt_parts) + len(ex)
                    j = 0
                    o_ = pb[:, bi_ * nq:(bi_ + 1) * nq]
                    for (lof, (rdep, rap)) in kt_parts:
                        ldep, lap = lof(c)
                        mm(o_, lap, rap, j == 0, j == np_ - 1, [ldep, rdep], [(pb, None)])
                        j += 1
                    for (ldep, lap, rdep, rap) in ex:
                        mm(o_, lap, rap, j == 0, j == np_ - 1, [ldep, rdep], [(pb, None)])
                        j += 1
                return pb

            def pv(batch, pb, idx0):
                p_ = pt[nxt("pt", 3)]
                w_ = len(batch) * nq
                act(p_[:, 0:w_], pb[:, 0:w_], AF.Exp, [(pb, None), (bias_t[b], None)], [(p_, None)], bias=bias_t[b][:, 0:1])
                for bi_, c in enumerate(batch):
                    i = idx0 + bi_
                    vdep, vap = v_of(c)
                    st_ = first and i == 0
                    sp_ = last and i == n - 1
                    r_ = p_[:, bi_ * nq:(bi_ + 1) * nq]
                    mm(po[:, c0:c0 + nq], vap, r_, st_, sp_, [vdep, (p_, None)], [(po, c0)])
                    mm(psu[:, c0:c0 + nq], ones_b[:], r_, st_, sp_, [(ones_b, None), (p_, None)], [(psu, c0)])

            pend = None
            idx = 0
            for batch in batches:
                pb = qk(batch)
                if pend is not None:
                    pv(*pend)
                pend = (batch, pb, idx)
                idx += len(batch)
            pv(*pend)
            if last:
                S.op("dve", lambda e: e.reciprocal(out=rinv[:, 0:nq], in_=psu[:, c0:c0 + nq]),
                     reads=[(psu, c0)], writes=[(rinv, None)])
                tt(out_ap, po[:, c0:c0 + nq], rinv[:, 0:nq], ALU.mult, [(po, c0), (rinv, None)], [out_dep])

        def load_kv_block(i, KTd, Vd, h, key0, nkeys, seq_idx, col0=0):
            kt_, v_ = ktb[i], vb[i]
            ksrc = KTd[seq_idx, h] if seq_idx is not None else KTd[h]
            vsrc = Vd[seq_idx, h] if seq_idx is not None else Vd[h]
            dma("sp", kt_[:, col0 * 128:col0 * 128 + nkeys], ksrc[:, key0:key0 + nkeys], reads=[(KTd, None)],
                writes=[(kt_, None)], dsem="ktb%d" % i)
            dma("sp", v_[:, col0:col0 + nkeys // 128, :], vsrc[:, key0 // 128:(key0 + nkeys) // 128, :],
                reads=[(Vd, None)], writes=[(v_, None)], dsem="vb%d" % i)

        blk_rot = [0]

        def mla_attention(c, prompt_idx, tok0):
            if prompt_idx is None:
                dma("sp", ropf[:], rope_fm[:, :, tok0:tok0 + T], writes=[(ropf, None)], dsem="ropf")
                dma("sp", krr[0:64, :], KR_s[0:64, :], reads=[(KR_s, None)], writes=[(krr, None)], dsem="krr")
            else:
                dma("sp", krr[0:64, 0:SEQ], KR_p[prompt_idx, 0:64, :], reads=[(KR_p, None)], writes=[(krr, None)], dsem="krr")
            def proj(h):
                b = h % 2
                pbq = PB[nxt("pb", 4)]
                for k in range(4):
                    mm(pbq[:, 0:T], wuq[:, k, h * 192:h * 192 + 128], cqT[:, k, :], k == 0, k == 3, [(wuq, None), (cqT, None)], [(pbq, 0)])
                for k in range(4):
                    mm(pbq[:, T:2 * T], wuq[:, k, h * 192 + 128:h * 192 + 256], cqT[:, k, :], k == 0, k == 3, [(wuq, None), (cqT, None)], [(pbq, 1)])
                act(qn[b][:], pbq[:, 0:T], AF.Copy, [(pbq, 0)], [(qn[b], None)], scale=MLA_SCALE)
                if prompt_idx is not None:
                    act(qr[b][0:64, :], pbq[0:64, T:2 * T], AF.Copy, [(pbq, 1)], [(qr[b], None)], scale=MLA_SCALE)
                else:
                    pbr = PB[nxt("pb", 4)]
                    for k in range(4):
                        mm(pbr[:, 0:T], wuqr[:, k, h * 64:h * 64 + 128], cqT[:, k, :], k == 0, k == 3, [(wuqr, None), (cqT, None)], [(pbr, None)])
                    tt(tmpf[0:64, 0:T], pbq[0:64, T:2 * T], ropf[:, 0, :], ALU.mult, [(pbq, 1), (ropf, None)], [(tmpf, None)])
                    tt(tmpg[0:64, 0:T], pbr[0:64, 0:T], ropf[:, 1, :], ALU.mult, [(pbr, None), (ropf, None)], [(tmpg, None)])
                    tt(tmpf[0:64, 0:T], tmpf[0:64, 0:T], tmpg[0:64, 0:T], ALU.add, [(tmpf, None), (tmpg, None)], [(tmpf, None)])
                    act(qr[b][0:64, :], tmpf[0:64, 0:T], AF.Copy, [(tmpf, None)], [(qr[b], None)], scale=MLA_SCALE)
                q_bias("mla", h, b, MLA_SCALE)

            proj(0)
            for h in range(H):
                b = h % 2
                if h + 1 < H:
                    proj(h + 1)
                acc = ACC[h % 2] + (0,)
                if prompt_idx is not None:
                    blocks = [(0, SEQ)]
                else:
                    blocks = [(0, 1280), (1280, 1280), (2560, 1280), (3840, 512)]
                for bi, (key0, nkeys) in enumerate(blocks):
                    i = blk_rot[0]
                    blk_rot[0] = (i + 1) % 2
                    if prompt_idx is not None:
                        load_kv_block(i, KTm_p, Vm_p, h, key0, nkeys, prompt_idx)
                    else:
                        load_kv_block(i, KTm_s, Vm_s, h, key0, nkeys, None)
                    parts = [
                        (lambda c_, i=i: ((ktb[i], None), ktb[i][:, c_ * 128:(c_ + 1) * 128]), ((qn[b], None), qn[b][:])),
                        (lambda c_, key0=key0: ((krr, None), krr[:, key0 + c_ * 128:key0 + (c_ + 1) * 128]), ((qr[b], None), qr[b][:])),
                    ]
                    attn_block(b, 0, T, list(range(nkeys // 128)), parts,
                               lambda c_, i=i: ((vb[i], None), vb[i][:, c_, :]), acc,
                               bi == 0, bi == len(blocks) - 1, oaT(h), (U, ("oa", h)))

        def na_attention(prompt_idx, gi):
            if prompt_idx is None:
                dma("sp", mkt[0:2, :], mk_d[:, gi * 2 * 7 * 128:(gi + 1) * 2 * 7 * 128], writes=[(mkt, None)], dsem="mkt")
            cur = {}

            def proj(h):
                if h % 2 == 0:
                    cur["sv"] = get_slab("in_qb", h // 2)
                if S.dry:
                    return
                slot, v = cur["sv"]
                m = h % 2
                b = h % 2
                pbq = PB[nxt("pb", 4)]
                for kc in range(KC):
                    mm(pbq[:, 0:T], v[:, kc, m * 128:(m + 1) * 128], hT[:, kc, :], kc == 0, kc == KC - 1,
                       [(slot, None), (hT, None)], [(pbq, None)])
                act(qn[b][:], pbq[:, 0:T], AF.Copy, [(pbq, None)], [(qn[b], None)], scale=NA_SCALE)
                q_bias("na", h, b, NA_SCALE)

            proj(0)
            for h in range(H):
                b = h % 2
                if h + 1 < H:
                    proj(h + 1)
                if S.dry:
                    continue
                i = blk_rot[0]
                blk_rot[0] = (i + 1) % 2
                if prompt_idx is not None:
                    load_kv_block(i, KTn_p, Vn_p, h, 0, SEQ, prompt_idx)
                    parts = [(lambda c_, i=i: ((ktb[i], None), ktb[i][:, c_ * 128:(c_ + 1) * 128]), ((qn[b], None), qn[b][:]))]
                    attn_block(b, 0, T, [0, 1], parts, lambda c_, i=i: ((vb[i], None), vb[i][:, c_, :]),
                               ACC[h % 2] + (0,), True, True, obT(h), (U, ("ob", h)))
                else:
                    load_kv_block(i, KTn_s, Vn_s, h, gi * 2 * 128, 1024, None)
                    load_kv_block(i, KTn_s, Vn_s, h, NA_ROWS * 64, 256, None, col0=8)
                    for rp in range(2):
                        pl = 2 * gi + rp
                        jlo, jhi = -2, 2
                        if pl == 0:
                            jhi = 3
                        if pl == 15:
                            jlo = -3
                        wch = [rp + j + 3 for j in range(jlo, jhi + 1)]
                        qap = qn[b][:, rp * 128:(rp + 1) * 128]
                        parts = [(lambda c_, i=i: ((ktb[i], None), ktb[i][:, c_ * 128:(c_ + 1) * 128]), ((qn[b], None), qap))]

                        def extra(c_, rp=rp, h=h):
                            if c_ >= 8:
                                return []
                            j = c_ - rp
                            return [((tbt, None), tbt[:, j, h, :], (ident_b, None), ident_b[:]),
                                    ((e2, None), e2[:], (mkt, None), mkt[:, (rp * 7 + j) * 128:(rp * 7 + j + 1) * 128])]

                        attn_block(b, rp * 128, 128, wch + [8, 9], parts,
                                   lambda c_, i=i: ((vb[i], None), vb[i][:, c_, :]),
                                   ACC[h % 2] + (rp * 128,), True, True,
                                   obT(h)[:, rp * 128:(rp + 1) * 128], (U, ("ob", h, rp)), extra=extra)

        def cq_stage():
            pbs = [PB[nxt("pb", 4)] for _ in range(2)]
            for s2 in range(2):
                slot, v = get_slab("in_cq", s2)
                if S.dry:
                    continue
                for m in range(2):
                    for kc in range(KC):
                        mm(pbs[s2][:, m * T:(m + 1) * T], v[:, kc, m * 128:(m + 1) * 128], hT[:, kc, :], kc == 0, kc == KC - 1,
                           [(slot, None), (hT, None)], [(pbs[s2], m)])
            if S.dry:
                return
            pn = PB[nxt("pb", 4)]
            for j in range(4):
                src = pbs[j // 2][:, (j % 2) * T:(j % 2 + 1) * T]
                act(sq[:], src, AF.Square, [(pbs[j // 2], j % 2)], [(sq, None)])
                mm(pn[:, 0:T], ones_b[:], sq[:], j == 0, j == 3, [(ones_b, None), (sq, None)], [(pn, None)])
            act(rinv[:], pn[:, 0:T], AF.Sqrt, [(pn, None), (epst, None)], [(rinv, None)], scale=1.0 / Q_LORA, bias=epst[:, 0:1])
            S.op("dve", lambda e: e.reciprocal(out=rinv[:], in_=rinv[:]), reads=[(rinv, None)], writes=[(rinv, None)])
            for j in range(4):
                src = pbs[j // 2][:, (j % 2) * T:(j % 2 + 1) * T]
                tt(cqT[:, j, :], src, rinv[:], ALU.mult, [(pbs[j // 2], j % 2), (rinv, None)], [(cqT, j)])

        def merge_stage():
            for r in range(8):
                halves = [(("in_ga", KC, lambda kc: [(hT, None)], lambda kc: hT[:, kc, :], PB[0]),
                           ("oa", 8, lambda kc: [(U, ("oa", kc))], lambda kc: oaT(kc), PB[2])),
                          (("in_gb", KC, lambda kc: [(hT, None)], lambda kc: hT[:, kc, :], PB[1]),
                           ("ob", 8, lambda kc: [(U, ("ob", kc)), (U, ("ob", kc, 0)), (U, ("ob", kc, 1))],
                            lambda kc: obT(kc), PB[3]))]
                for half in halves:
                    for (name, nk, rdeps, rhs_of, bank) in half:
                        slot, v = get_slab(name, r)
                        if S.dry:
                            continue
                        for m in range(2):
                            for kc in range(nk):
                                mm(bank[:, m * T:(m + 1) * T], v[:, kc, m * 128:(m + 1) * 128], rhs_of(kc), kc == 0, kc == nk - 1,
                                   [(slot, None)] + rdeps(kc), [(bank, m)])
                if S.dry:
                    continue
                act(tmpf[:], PB[0][:], AF.Sigmoid, [(PB[0], None)], [(tmpf, None)])
                tt(tmpf[:], PB[2][:], tmpf[:], ALU.mult, [(PB[2], None), (tmpf, None)], [(tmpf, None)])
                act(tmpg[:], PB[1][:], AF.Sigmoid, [(PB[1], None)], [(tmpg, None)])
                tt(tmpg[:], PB[3][:], tmpg[:], ALU.mult, [(PB[3], None), (tmpg, None)], [(tmpg, None)])
                for m in range(2):
                    tt(mgT(2 * r + m), tmpf[:, m * T:(m + 1) * T], tmpg[:, m * T:(m + 1) * T], ALU.add,
                       [(tmpf, None), (tmpg, None)], [(U, ("mg", 2 * r + m))])

        def resid_tm(name, lhs_of, nkc_total, gi_):
            def evac(blk, pbs, wdt):
                for t_ in range(2):
                    tt(tmpf[:], pbs[t_][:], gtb[gi_][:, blk * 512:(blk + 1) * 512], ALU.mult,
                       [(pbs[t_], None), (gtb[gi_], None)], [(tmpf, None)])
                    tt(xt[:, t_, blk * 512:(blk + 1) * 512], xt[:, t_, blk * 512:(blk + 1) * 512], tmpf[:], ALU.add,
                       [(xt, None), (tmpf, None)], [(xt, None)])
            tm_matmul(name, 4, lhs_of, nkc_total, evac)

        def ffn_stage():
            for j in range(NFF):
                slot, v = get_slab("gu", j)
                if S.dry:
                    continue
                pb = PB[nxt("pb", 4)]
                for m in range(2):
                    for kc in range(KC):
                        mm(pb[:, m * T:(m + 1) * T], v[:, kc, m * 128:(m + 1) * 128], hT[:, kc, :], kc == 0, kc == KC - 1,
                           [(slot, None), (hT, None)], [(pb, m)])
                act(tmpf[:, 0:T], pb[:, 0:T], AF.Silu, [(pb, 0)], [(tmpf, None)])
                tt(actT(j), tmpf[:, 0:T], pb[:, T:2 * T], ALU.mult, [(tmpf, None), (pb, 1)], [(U, ("act", j))])

        def final_out(dst_ap):
            for t_ in range(2):
                act(xn[:], xt[:, t_, :], AF.Square, [(xt, None)], [(xn, None), (small, 4)], accum_out=small[:, 4:5])
                rstd_of(small[:, 4:5], D, small[:, 5:6], [(small, 4)], [(small, 5)])
                act(ystage[:], xt[:, t_, :], AF.Copy, [(xt, None), (small, 5)], [(ystage, None)], scale=small[:, 5:6])
                tt(ystage[:], ystage[:], gfb[:], ALU.mult, [(ystage, None), (gfb, None)], [(ystage, None)])
                dma("pool", dst_ap[t_ * 128:(t_ + 1) * 128, :], ystage[:], reads=[(ystage, None)], dsem="ystage")

        def u_barrier():
            S.op("dve", lambda e: e.memset(small[:, 20:21], 0.0), writes=[(U, None), (small, 20)])

        def main_group(c, src_ap, dst_ap, prompt_idx, gi):
            load_x(src_ap)
            norm_mod(c, 0)
            u_barrier()
            cq_stage()
            mla_attention(c, prompt_idx, None if prompt_idx is not None else gi * T)
            na_attention(prompt_idx, gi)
            merge_stage()
            resid_tm("out", lambda kk, t_: ((U, ("mg", kk)), mgT(kk)[:, t_ * 128:(t_ + 1) * 128]), KC, 0)
            norm_mod(c, 1)
            u_barrier()
            ffn_stage()
            resid_tm("down", lambda kk, t_: ((U, ("act", kk)), actT(kk)[:, t_ * 128:(t_ + 1) * 128]), NFF, 1)
            final_out(dst_ap)

        def program():
            if not S.dry:
                prologue()
                if stage >= 1:
                    convert_weights()
            if stage < 3:
                return
            import os
            sub = int(os.environ.get("KSUB", "9"))
            for s_ in range(4 if sub >= 9 else 1):
                if not S.dry:
                    load_x(xp[s_])
                    norm_mod(0, 0)
                if sub >= 1:
                    kv_latent(0, None, s_, 0)
                if sub >= 3:
                    kv_na(s_, 0)
            if stage < 6:
                return
            if not S.dry:
                kv_ctx()
            for g in range(16):
                if not S.dry:
                    load_x(xs_full[g * T:(g + 1) * T, :])
                    norm_mod(1, 0)
                kv_latent(1, g * T, None, g * T)
            for g in range(NA_ROWS // 4):
                if not S.dry:
                    load_x(xs_na[g * T:(g + 1) * T, :])
                    norm_mod(1, 0)
                kv_na(None, g * T)
            store_q[0] = "pool"
            if not S.dry:
                build_gates(0)
            for s_ in range(4):
                main_group(0, xp[s_], yp[s_], s_, None)
            if not S.dry:
                build_gates(1)
            for g in range(8):
                main_group(1, xs_own[g * T:(g + 1) * T, :], ys[g * T:(g + 1) * T, :], None, g)

        S.dry = True
        program()
        S.dry = False
        store_q[0] = "sp"
        stream["pos"] = 0
        program()
        S.emit(st)
        print("ops:", {e: len(S.ops[e]) for e in ENGS}, "dsems:", len(S.dsem_counts), flush=True)
    return nc


def _rope_tables():
    t = np.arange(4096)
    row = (t // GRID_W).astype(np.float32)
    col = (t % GRID_W).astype(np.float32)
    nf = ROPE // 4
    inv = (np.float32(10000.0) ** (-np.arange(nf, dtype=np.float32) / nf)).astype(np.float32)
    ang = np.concatenate([row[:, None] * inv, col[:, None] * inv], axis=-1).astype(np.float32)
    return np.cos(ang).astype(np.float32), np.sin(ang).astype(np.float32)


def _na_tables(rpb):
    qc = np.arange(64)
    kc = np.arange(64)
    q_start = np.clip(qc - 8, 0, 48)
    inw = (kc[None, :] >= q_start[:, None]) & (kc[None, :] < q_start[:, None] + 16)
    coff = np.clip(kc[None, :] - qc[:, None] + 15, 0, 30)
    tb = np.zeros((2, 64, 7, H, 2, 64), np.float32)
    for j in range(7):
        for qro in range(2):
            for kro in range(2):
                dr = 2 * (j - 3) + kro - qro
                if abs(dr) > 7:
                    continue
                g = rpb[:, dr + 7, :][:, coff]
                g = np.where(inw[None], g, np.float32(NEG))
                tb[qro, :, j, :, kro, :] = np.transpose(g, (1, 0, 2))
    return tb.reshape(128, 7 * H * 128)


def _na_masks(half):
    mk = np.zeros((2, 16, 7, 2, 64), np.float32)
    for pl in range(16):
        P = pl + 16 * half
        for j in range(7):
            p = P + j - 3
            for kro in range(2):
                kr = 2 * p + kro
                for qro in range(2):
                    r = 2 * P + qro
                    r0 = min(max(r - 4, 0), 56)
                    ok = (0 <= kr < 64) and (r0 <= kr < r0 + 8)
                    mk[kro, pl, j, qro, :] = 0.0 if ok else NEG
    return mk.reshape(2, 16 * 7 * 128).astype(ml_dtypes.bfloat16)


_NC_CACHE = {}


def kernel(x_prompt, x_sample, cache_mla_ckv, cache_mla_krope, cache_na_k, cache_na_v, c, c_ctx,
           w_mod, b_mod, norm1_g, w_in, q_norm_g, kv_norm_g, w_uq, w_uk, w_uv, rpb,
           w_oa, w_ob, w_out, norm2_g, w_gu, w_down, norm_f_g):
    f = lambda a: np.ascontiguousarray(np.asarray(a, dtype=np.float32))
    x_prompt, x_sample = f(x_prompt), f(x_sample)
    if "nc" not in _NC_CACHE:
        _NC_CACHE["nc"] = build_program()
    nc = _NC_CACHE["nc"]
    cos, sin = _rope_tables()
    rope_tm = np.concatenate([cos, sin], axis=1)
    fm = lambda v, n: f(v).reshape(n, 128).T
    tb = _na_tables(f(rpb)[0])
    e2 = np.zeros((128, 128), np.float32)
    e2[0, :64] = 1.0
    e2[1, 64:] = 1.0
    shared = {
        "w_in": f(w_in)[0], "w_oa": f(w_oa)[0], "w_ob": f(w_ob)[0], "w_out": f(w_out)[0], "w_gu": f(w_gu)[0],
        "w_down": f(w_down)[0], "w_mod": f(w_mod)[0], "w_uq": f(w_uq)[0], "w_uk": f(w_uk)[0], "w_uv": f(w_uv)[0],
        "kvg_b": np.ascontiguousarray(np.broadcast_to(f(kv_norm_g)[0][None, :], (128, KV_LORA))),
        "gf_b": np.ascontiguousarray(np.broadcast_to(f(norm_f_g)[None, :], (128, D))),
        "ident": np.eye(128, dtype=np.float32), "rope_tm": rope_tm, "tb": tb,
        "e2": e2.astype(ml_dtypes.bfloat16),
    }
    in_maps = []
    for core in range(8):
        b, half = core // 2, core % 2
        cond = np.stack([fm(c_ctx, 16), fm(f(c)[b], 16)], axis=-1).reshape(128, 32)
        vecs = np.concatenate([cond, fm(f(norm1_g)[0], 16), fm(f(norm2_g)[0], 16), fm(f(q_norm_g)[0], 4),
                               fm(f(b_mod)[0], 96)], axis=1)
        xs = x_sample[b]
        xna = np.zeros((NA_ROWS * 64, D), np.float32)
        if half == 0:
            xna[6 * 64:] = xs[0:38 * 64]
        else:
            xna[:38 * 64] = xs[26 * 64:]
        own = slice(half * 2048, (half + 1) * 2048)
        rfm = np.stack([np.concatenate([cos[own].T, cos[own].T], 0), np.concatenate([sin[own].T, sin[own].T], 0)], axis=1)
        m = dict(shared)
        m.update({
            "xp": x_prompt[core * 4:(core + 1) * 4], "xs_full": xs, "xs_na": xna, "xs_own": np.ascontiguousarray(xs[own]),
            "c_ckv": f(cache_mla_ckv)[b, 0], "c_kr": f(cache_mla_krope)[b, 0],
            "c_nak": f(cache_na_k)[b, 0].reshape(256, H * HD), "c_nav": f(cache_na_v)[b, 0].reshape(256, H * HD),
            "vecs": np.ascontiguousarray(vecs), "rope_fm": np.ascontiguousarray(rfm.astype(np.float32)),
            "mk": _na_masks(half),
        })
        in_maps.append(m)
    if _NC_CACHE.get("debug_hook") is not None:
        return _NC_CACHE["debug_hook"](in_maps)
    res = run_bass_kernel_spmd(nc, in_maps, core_ids=list(range(8))).results
    y_prompt = np.concatenate([r["yp"] for r in res], axis=0)
    y_sample = np.stack([np.concatenate([res[2 * b]["ys"], res[2 * b + 1]["ys"]], axis=0) for b in range(4)], axis=0)
    ckv = np.concatenate([r["o_ckv"] for r in res], axis=0)[:, None]
    kr = np.concatenate([r["o_kr"] for r in res], axis=0)[:, None]
    nak = np.concatenate([r["o_nak"] for r in res], axis=0).reshape(32, 1, SEQ, H, HD)
    nav = np.concatenate([r["o_nav"] for r in res], axis=0).reshape(32, 1, SEQ, H, HD)
    return (y_prompt.astype(np.float32), y_sample.astype(np.float32), ckv.astype(np.float32), kr.astype(np.float32),
            nak.astype(np.float32), nav.astype(np.float32))
```

```python
import numpy as np
from contextlib import ExitStack
import ml_dtypes
import concourse.bass as bass
import concourse.mybir as mybir
from concourse.bass_utils import run_bass_kernel_spmd

F32 = mybir.dt.float32
BF16 = mybir.dt.bfloat16
ALU = mybir.AluOpType
AF = mybir.ActivationFunctionType
AX = mybir.AxisListType

D = 2048
KC = 16
T = 256
SEQ = 256
GRID_W = 64
Q_LORA, KV_LORA, ROPE, NOPE, VD = 512, 256, 64, 128, 128
H = 8
HD = 128
DFF = 5632
NFF = 44
IN_COLS = 8000
O_CQ, O_CKV, O_KR, O_QB, O_KB, O_VB, O_GA, O_GB = 0, 512, 768, 832, 1856, 2880, 3904, 5952
EPS = 1e-6
MLA_SCALE = float((NOPE + ROPE) ** -0.5)
NA_SCALE = float(HD ** -0.5)
NEG = -30000.0
NK_MLA = 4096 + 256
NA_ROWS = 44
NK_NA = NA_ROWS * 64 + 256
SLAB = 4096
PREFETCH = 2
NSLOT = 3


class TT:
    __slots__ = ("name", "h", "recs", "psum")

    def __init__(self, name, h, psum=False):
        self.name = name
        self.h = h
        self.recs = {}
        self.psum = psum

    def __getitem__(self, k):
        return self.h[k]


class Op:
    __slots__ = ("eng", "fn", "deps", "is_dma", "dsem", "dval", "sig", "sigidx")

    def __init__(self, eng, fn, is_dma):
        self.eng = eng
        self.fn = fn
        self.deps = []
        self.is_dma = is_dma
        self.dsem = None
        self.dval = 0
        self.sig = False
        self.sigidx = None


ENGS = ("pe", "act", "dve", "pool", "sp")


class Sched:
    def __init__(self, nc):
        self.nc = nc
        self.ops = {e: [] for e in ENGS}
        self.dry = False
        self.dsem_counts = {}

    def _conf(self, t, key, write, op):
        recs = t.recs
        if key is None:
            keys = list(recs.keys())
        else:
            keys = [k for k in (key, None) if k in recs]
        for k in keys:
            w, rd, drd = recs[k]
            if w is not None:
                op.deps.append((w, "waw" if write else "raw"))
            if write:
                for r in rd.values():
                    op.deps.append((r, "war"))
                for r in drd:
                    op.deps.append((r, "war"))

    def _upd(self, t, key, write, op):
        recs = t.recs
        if write:
            if key is None:
                recs.clear()
            recs[key] = [op, {}, []]
        else:
            if key not in recs:
                recs[key] = [None, {}, []]
            if op.is_dma:
                recs[key][2].append(op)
            else:
                recs[key][1][op.eng] = op

    def op(self, eng, fn, reads=(), writes=(), dma=False, dsem=None):
        if self.dry:
            return None
        o = Op(eng, fn, dma)
        if any(t.psum for (t, k) in reads) or any(t.psum for (t, k) in writes):
            pw = []
            for (t, k) in list(reads) + list(writes):
                if t.psum and all(t is not q for (q, _) in pw):
                    pw.append((t, None))
            reads = [(t, k) for (t, k) in reads if not t.psum]
            writes = [(t, k) for (t, k) in writes if not t.psum] + pw
        for (t, k) in reads:
            self._conf(t, k, False, o)
        for (t, k) in writes:
            self._conf(t, k, True, o)
        for (t, k) in reads:
            self._upd(t, k, False, o)
        for (t, k) in writes:
            self._upd(t, k, True, o)
        if dma:
            c = self.dsem_counts.get(dsem, 0) + 16
            self.dsem_counts[dsem] = c
            o.dsem = dsem
            o.dval = c
        self.ops[eng].append(o)
        return o

    def emit(self, stack):
        nc = self.nc
        for e in ENGS:
            for o in self.ops[e]:
                nd = []
                seen = set()
                for (p, kind) in o.deps:
                    if id(p) in seen or p is o:
                        continue
                    if (not p.is_dma) and p.eng == o.eng and not o.is_dma:
                        if o.eng == "pe":
                            continue
                    seen.add(id(p))
                    nd.append(p)
                    if not p.is_dma:
                        p.sig = True
                o.deps = nd
        EPOCH = 16000
        nep = {}
        for e in ENGS:
            c = 0
            for o in self.ops[e]:
                if o.sig and not o.is_dma:
                    o.sigidx = (c // EPOCH, c % EPOCH + 1)
                    c += 1
            nep[e] = c // EPOCH + 1
        csem = {(e, i): stack.enter_context(nc.semaphore("cs_%s%d" % (e, i)))
                for e in ENGS for i in range(nep[e])}
        dsem = {k: stack.enter_context(nc.semaphore("ds_%d" % i))
                for i, k in enumerate(self.dsem_counts.keys())}
        block = stack.enter_context(nc.Block())
        sched = self

        def body(e):
            def f(eng):
                known = {}
                for o in sched.ops[e]:
                    need = {}
                    for p in o.deps:
                        if p.is_dma:
                            s, v = dsem[p.dsem], p.dval
                        else:
                            s, v = csem[(p.eng, p.sigidx[0])], p.sigidx[1]
                        key = id(s)
                        if known.get(key, 0) >= v:
                            continue
                        if key not in need or need[key][1] < v:
                            need[key] = (s, v)
                    for key, (s, v) in need.items():
                        eng.wait_ge(s, v)
                        known[key] = v
                    ins = o.fn(eng)
                    if o.is_dma:
                        ins.then_inc(dsem[o.dsem], 16)
                    elif o.sig:
                        ins.then_inc(csem[(e, o.sigidx[0])], 1)
                if e == "sp":
                    for k, c in sched.dsem_counts.items():
                        eng.wait_ge(dsem[k], c)
            return f

        block.tensor(body("pe"))
        block.scalar(body("act"))
        block.vector(body("dve"))
        block.gpsimd(body("pool"))
        block.sync(body("sp"))


def slab_catalogue():
    cat = {}

    def fm(wname, c0, ncols, nkc=16, width=256):
        return [(wname, 0, nkc, [(c, width)]) for c in range(c0, c0 + ncols, width)]

    def tm(wname, c0, ncols, K, width=512):
        out = []
        for c in range(c0, c0 + ncols, width):
            w = min(width, c0 + ncols - c)
            for k0 in range(0, K // 128, 8):
                out.append((wname, k0, min(8, K // 128 - k0), [(c, w)]))
        return out

    cat["in_cq"] = fm("w_in", O_CQ, 512)
    cat["in_qb"] = fm("w_in", O_QB, 1024)
    cat["in_ga"] = fm("w_in", O_GA, 2048)
    cat["in_gb"] = fm("w_in", O_GB, 2048)
    cat["in_kv"] = tm("w_in", O_CKV, 320, D, width=320)
    cat["in_kb"] = tm("w_in", O_KB, 1024, D)
    cat["in_vb"] = tm("w_in", O_VB, 1024, D)
    cat["oa"] = fm("w_oa", 0, 2048, nkc=8)
    cat["ob"] = fm("w_ob", 0, 2048, nkc=8)
    cat["out"] = tm("w_out", 0, 2048, D)
    cat["gu"] = [("w_gu", 0, 16, [(j * 128, 128), (DFF + j * 128, 128)]) for j in range(NFF)]
    cat["down"] = tm("w_down", 0, 2048, DFF)
    return cat


CAT = slab_catalogue()
SLAB_IDS = {}
_n = 0
for _k, _v in CAT.items():
    for _i in range(len(_v)):
        SLAB_IDS[(_k, _i)] = _n
        _n += 1
NSLAB = _n


def build_program(stage=99):
    nc = bass.Bass("TRN2", target_bir_lowering=False)
    S = Sched(nc)

    def din(name, shape, dt=F32):
        return nc.dram_tensor(name, list(shape), dt, kind="ExternalInput").ap()

    def dout(name, shape):
        return nc.dram_tensor(name, list(shape), F32, kind="ExternalOutput").ap()

    def dscr(name, shape, dt=BF16):
        return TT(name, nc.dram_tensor("d_" + name, list(shape), dt, kind="Internal").ap())

    xp = din("xp", [4, SEQ, D])
    xs_full = din("xs_full", [4096, D])
    xs_na = din("xs_na", [NA_ROWS * 64, D])
    xs_own = din("xs_own", [2048, D])
    c_ckv = din("c_ckv", [256, KV_LORA])
    c_kr = din("c_kr", [256, ROPE])
    c_nak = din("c_nak", [256, H * HD])
    c_nav = din("c_nav", [256, H * HD])
    W = {
        "w_in": din("w_in", [D, IN_COLS]), "w_oa": din("w_oa", [1024, D]), "w_ob": din("w_ob", [1024, D]),
        "w_out": din("w_out", [D, D]), "w_gu": din("w_gu", [D, 2 * DFF]), "w_down": din("w_down", [DFF, D]),
    }
    w_mod = din("w_mod", [D, 6 * D])
    w_uq = din("w_uq", [Q_LORA, H * 192])
    w_uk = din("w_uk", [KV_LORA, H * NOPE])
    w_uv = din("w_uv", [KV_LORA, H * VD])
    vecs = din("vecs", [128, 16 * 2 + 16 + 16 + 4 + 96])
    kvg_b_d = din("kvg_b", [128, KV_LORA])
    gf_b_d = din("gf_b", [128, D])
    ident_d = din("ident", [128, 128])
    rope_tm = din("rope_tm", [4096, 64])
    rope_fm = din("rope_fm", [64, 2, 2048])
    tb_d = din("tb", [128, 7 * H * 128])
    mk_d = din("mk", [2, 16 * 7 * 128], BF16)
    e2_d = din("e2", [128, 128], BF16)

    yp = dout("yp", [4, SEQ, D])
    ys = dout("ys", [2048, D])
    o_ckv = dout("o_ckv", [4, SEQ, KV_LORA])
    o_kr = dout("o_kr", [4, SEQ, ROPE])
    o_nak = dout("o_nak", [4, SEQ, H * HD])
    o_nav = dout("o_nav", [4, SEQ, H * HD])

    WB = dscr("WB", [NSLAB, 128, SLAB])
    CVT = [TT("cvt%d" % i, None) for i in range(8)]
    WMD = TT("wmod_done", None)
    KTm_p = dscr("KTm_p", [4, H, 128, SEQ]); KR_p = dscr("KR_p", [4, 128, SEQ]); Vm_p = dscr("Vm_p", [4, SEQ, H * VD])
    KTn_p = dscr("KTn_p", [4, H, 128, SEQ]); Vn_p = dscr("Vn_p", [4, SEQ, H * HD])
    KTm_s = dscr("KTm_s", [H, 128, NK_MLA]); KR_s = dscr("KR_s", [128, NK_MLA]); Vm_s = dscr("Vm_s", [NK_MLA, H * VD])
    KTn_s = dscr("KTn_s", [H, 128, NK_NA]); Vn_s = dscr("Vn_s", [NK_NA, H * HD])

    with ExitStack() as st:
        def sb(name, shape, dt):
            return TT(name, st.enter_context(nc.sbuf_tensor("s_" + name, list(shape), dt)))

        def ps(name, shape, dt):
            return TT(name, st.enter_context(nc.psum_tensor("p_" + name, list(shape), dt)), psum=True)

        ident_f = sb("ident_f", [128, 128], F32)
        ident_b = sb("ident_b", [128, 128], BF16)
        ones_f = sb("ones_f", [128, 128], F32)
        ones_b = sb("ones_b", [128, 128], BF16)
        e2 = sb("e2", [128, 128], BF16)
        vec = sb("vec", [128, 164], F32)
        silu_c = sb("silu_c", [128, 16, 2], F32)
        mrow = sb("mrow", [2, 512], F32)
        modfm = sb("modfm", [128, 96, 2], F32)
        AB = sb("AB", [128, 2, 4, 16], F32)
        gtb = [sb("gt1b", [128, D], F32), sb("gt2b", [128, D], F32)]
        gfb = sb("gfb", [128, D], F32)
        kvgb = sb("kvgb", [128, KV_LORA], F32)
        wuq = sb("wuq", [128, 4, H * 192 + 64], BF16)
        wuqr = sb("wuqr", [128, 4, H * 64 + 64], BF16)
        wuk = sb("wuk", [128, 2, H * NOPE], BF16)
        wuv = sb("wuv", [128, 2, H * VD], BF16)
        tbt = sb("tbt", [128, 7, H, 128], BF16)
        kmax = {k: sb("kmax_" + k, [128, H], F32) for k in ("mla", "na")}
        kmaxr = sb("kmaxr", [128, 1], F32)
        xt = sb("xt", [128, 2, D], F32)
        xn = sb("xn", [128, D], BF16)
        xn2 = sb("xn2", [128, D], BF16)
        hT = sb("hT", [128, KC, T], BF16)
        cqT = sb("cqT", [128, 4, T], BF16)
        U = sb("U", [128, NFF * T], BF16)
        slabs = [sb("slab%d" % i, [128, SLAB], BF16) for i in range(NSLOT)]
        ystage = sb("ystage", [128, D], F32)
        tmpf = sb("tmpf", [128, 512], F32)
        tmpg = sb("tmpg", [128, 512], F32)
        small = sb("small", [128, 64], F32)
        epst = sb("epst", [128, 1], F32)
        ckvst = sb("ckvst", [128, 2, 320], F32)
        tmb = sb("tmb", [128, 1024], BF16)
        ktst = sb("ktst", [128, H, T], BF16)
        krst = sb("krst", [128, T], BF16)
        tmk = sb("tmk", [128, 384], BF16)
        ckvT = sb("ckvT", [128, 2, T], BF16)
        vst = sb("vst", [128, 2, H * VD], BF16)
        sq = sb("sq", [128, T], BF16)
        sqr = sb("sqr", [128, T], BF16)
        ropt = sb("ropt", [128, 2, 64], F32)
        ropf = sb("ropf", [64, 2, T], F32)
        ropw = sb("ropw", [128, 6, 32], F32)
        mkt = sb("mkt", [128, 2 * 7 * 128], BF16)
        qn = [sb("qn%d" % i, [128, T], BF16) for i in range(2)]
        qr = [sb("qr%d" % i, [128, T], BF16) for i in range(2)]
        bias_t = [sb("bias%d" % i, [128, 2], F32) for i in range(2)]
        ktb = [sb("ktb%d" % i, [128, 1280], BF16) for i in range(2)]
        krb = [sb("krb%d" % i, [128, 1280], BF16) for i in range(2)]
        vb = [sb("vb%d" % i, [128, 10, 128], BF16) for i in range(2)]
        pt = [sb("pt%d" % i, [128, 512], BF16) for i in range(3)]
        rinv = sb("rinv", [128, T], F32)
        PB = [ps("pb%d" % i, [128, 512], F32) for i in range(6)]
        PBT = [ps("pbt%d" % i, [128, 1024], BF16) for i in range(2)]

        ACC = [(PB[4], PB[5]), (TT("pbt0f", PBT[0][:, :].bitcast(F32), psum=True), TT("pbt1f", PBT[1][:, :].bitcast(F32), psum=True))]
        ACC[1][0].recs = PBT[0].recs
        ACC[1][1].recs = PBT[1].recs
        NB = [PBT[0], PBT[1], TT("pb4b", PB[4][:, :].bitcast(BF16), psum=True), TT("pb5b", PB[5][:, :].bitcast(BF16), psum=True)]
        NB[2].recs = PB[4].recs
        NB[3].recs = PB[5].recs
        print("sbuf bytes remaining", nc.sbuf_bytes_remaining, flush=True)
        oaT = lambda h: U[:, h * T:(h + 1) * T]
        obT = lambda h: U[:, (H + h) * T:(H + h + 1) * T]
        mgT = lambda kc: U[:, (16 + kc) * T:(16 + kc + 1) * T]
        actT = lambda j: U[:, j * T:(j + 1) * T]

        rot = {"pb": 0, "pbt": 0, "pt": 0}
        store_q = ["sp"]

        def nxt(kind, n):
            i = rot[kind]
            rot[kind] = (i + 1) % n
            return i

        def dma(q, out, in_, reads=(), writes=(), dsem=None, **kw):
            S.op(q, lambda e: e.dma_start(out=out, in_=in_, **kw), reads=reads, writes=writes, dma=True, dsem=dsem)

        def mm(out, lhsT, rhs, start, stop, reads, writes):
            S.op("pe", lambda e: e.matmul(out, lhsT=lhsT, rhs=rhs, start=start, stop=stop), reads=reads, writes=writes)

        def tr(out, in_, idn, reads, writes):
            S.op("pe", lambda e: e.transpose(out=out, in_=in_, identity=idn), reads=reads, writes=writes)

        def act(out, in_, func, reads, writes, **kw):
            S.op("act", lambda e: e.activation(out=out, in_=in_, func=func, **kw), reads=reads, writes=writes)

        def tt(out, in0, in1, op, reads, writes, eng="dve"):
            S.op(eng, lambda e: e.tensor_tensor(out=out, in0=in0, in1=in1, op=op), reads=reads, writes=writes)

        def ts(out, in0, s1, s2, op0, op1, reads, writes, eng="dve"):
            if s2 is None:
                S.op(eng, lambda e: e.tensor_scalar(out=out, in0=in0, scalar1=s1, scalar2=None, op0=op0),
                     reads=reads, writes=writes)
            else:
                S.op(eng, lambda e: e.tensor_scalar(out=out, in0=in0, scalar1=s1, scalar2=s2, op0=op0, op1=op1),
                     reads=reads, writes=writes)

        def cp(out, in_, reads, writes, eng="dve"):
            S.op(eng, lambda e: e.tensor_copy(out=out, in_=in_), reads=reads, writes=writes)

        stream = {"list": [], "pos": 0, "issued": 0}

        def slab_view(spec):
            _, k0, nkc, cols = spec
            wdt = sum(c[1] for c in cols)
            return nkc, wdt

        def issue_load(i):
            name, idx = stream["list"][i]
            spec = CAT[name][idx]
            nkc, wdt = slab_view(spec)
            sid = SLAB_IDS[(name, idx)]
            slot = slabs[i % NSLOT]
            n = nkc * wdt
            dma("sp", slot[:, 0:n], WB[sid, :, 0:n], reads=[(WB, sid)], writes=[(slot, None)], dsem="slab%d" % (i % NSLOT))

        def get_slab(name, idx):
            i = stream["pos"]
            stream["pos"] += 1
            if S.dry:
                stream["list"].append((name, idx))
                return None, None
            assert stream["list"][i] == (name, idx), (i, stream["list"][i], name, idx)
            while stream["issued"] <= min(i + PREFETCH, len(stream["list"]) - 1):
                issue_load(stream["issued"])
                stream["issued"] += 1
            slot = slabs[i % NSLOT]
            nkc, wdt = slab_view(CAT[name][idx])
            return slot, slot[:, 0:nkc * wdt].rearrange("p (k c) -> p k c", c=wdt)

        def prologue():
            dma("sp", ident_f[:], ident_d, writes=[(ident_f, None)], dsem="c_ident")
            dma("sp", vec[:], vecs, writes=[(vec, None)], dsem="c_vec")
            dma("sp", kvgb[:], kvg_b_d, writes=[(kvgb, None)], dsem="c_kvgb")
            dma("sp", gfb[:], gf_b_d, writes=[(gfb, None)], dsem="c_gfb")
            dma("sp", e2[:], e2_d, writes=[(e2, None)], dsem="c_e2")
            cp(ident_b[:], ident_f[:], [(ident_f, None)], [(ident_b, None)])
            S.op("dve", lambda e: e.memset(ones_f[:], 1.0), writes=[(ones_f, None)])
            S.op("dve", lambda e: e.memset(epst[:], EPS), writes=[(epst, None)])
            S.op("dve", lambda e: e.memset(ones_b[:], 1.0), writes=[(ones_b, None)])
            for k in list(kmax.values()) + [kmaxr]:
                S.op("dve", lambda e, k=k: e.memset(k[:], 0.0), writes=[(k, None)])
            for z_ in (wuq, wuqr, tmk, mkt, qr[0], qr[1]):
                S.op("pool", lambda e, z_=z_: e.memset(z_[:], 0.0), writes=[(z_, None)])
            dma("pool", wuk[:], w_uk.rearrange("(k p) m -> p k m", p=128), writes=[(wuk, None)], dsem="c_wuk")
            dma("pool", wuv[:], w_uv.rearrange("(k p) m -> p k m", p=128), writes=[(wuv, None)], dsem="c_wuv")
            dma("pool", tbt[:].rearrange("p a h k -> p a (h k)"), tb_d.rearrange("p (a x) -> p a x", a=7),
                writes=[(tbt, None)], dsem="c_tbt")
            for kc in range(4):
                stg = slabs[kc % NSLOT]
                stf = stg[:, :].bitcast(F32)
                dma("sp", stf[:, 0:1536], w_uq[kc * 128:(kc + 1) * 128, :], writes=[(stg, None)], dsem="wq%d" % (kc % NSLOT))
                src = stf[:, 0:1536].rearrange("p (h c) -> p h c", c=192)
                g = vec[:, 64 + kc:65 + kc]
                wq3 = wuq[:, kc, 0:H * 192].rearrange("p (h c) -> p h c", c=192)
                wr3 = wuqr[:, kc, 0:H * 64].rearrange("p (h c) -> p h c", c=64)
                ts(wq3, src, g, None, ALU.mult, None, [(stg, None), (vec, None), (wuq, None)], [(wuq, kc)])
                ts(wr3[:, :, 0:32], src[:, :, 160:192], g, -1.0, ALU.mult, ALU.mult, [(stg, None), (vec, None), (wuqr, None)], [(wuqr, (kc, 0))])
                ts(wr3[:, :, 32:64], src[:, :, 128:160], g, None, ALU.mult, None, [(stg, None), (vec, None), (wuqr, None)], [(wuqr, (kc, 1))])
            act(silu_c[:].rearrange("p k c -> p (k c)"), vec[:, 0:32], AF.Silu, [(vec, None)], [(silu_c, None)])
            li = 0
            for cb in range(24):
                pb = PB[cb % 2]
                for ks in range(4):
                    stg = slabs[li % NSLOT]
                    stf = stg[:, :].bitcast(F32).rearrange("p (k c) -> p k c", c=512)
                    dma("sp", stf, w_mod[ks * 512:(ks + 1) * 512, cb * 512:(cb + 1) * 512].rearrange("(k p) c -> p k c", p=128),
                        writes=[(stg, None)] + ([(WMD, None)] if li == 95 else []), dsem="wq%d" % (li % NSLOT))
                    li += 1
                    for k in range(4):
                        kc = ks * 4 + k
                        mm(pb[0:2, :], silu_c[:, kc, :], stf[:, k, :], kc == 0, kc == KC - 1,
                           [(stg, None), (silu_c, None)], [(pb, None)])
                cp(mrow[:], pb[0:2, :], [(pb, None)], [(mrow, None)])
                pT = PB[2 + cb % 2]
                for j in range(4):
                    tr(pT[:, 2 * j:2 * j + 2], mrow[:, j * 128:(j + 1) * 128], ident_f[0:2, 0:2],
                       [(mrow, None), (ident_f, None)], [(pT, None)])
                for c in range(2):
                    tt(modfm[:, 4 * cb:4 * cb + 4, c], pT[:, 0:8].rearrange("p (j c) -> p j c", c=2)[:, :, c],
                       vec[:, 68 + 4 * cb:72 + 4 * cb], ALU.add, [(pT, None), (vec, None)], [(modfm, (cb, c))])
            for c in range(2):
                for j, (ps_, pg) in enumerate(((1, 32), (4, 48))):
                    S.op("dve", lambda e, c=c, j=j, ps_=ps_, pg=pg: e.scalar_tensor_tensor(
                        out=AB[:, c, 2 * j, :], in0=modfm[:, ps_ * 16:(ps_ + 1) * 16, c], scalar=1.0,
                        in1=vec[:, pg:pg + 16], op0=ALU.add, op1=ALU.mult),
                        reads=[(modfm, None), (vec, None)], writes=[(AB, (c, 2 * j))])
                    sh = ps_ - 1
                    cp(AB[:, c, 2 * j + 1, :], modfm[:, sh * 16:(sh + 1) * 16, c], [(modfm, None)], [(AB, (c, 2 * j + 1))])

        def convert_weights():
            order = [(n, i) for n in ("in_kv", "in_kb", "in_vb", "in_cq", "in_qb") for i in range(len(CAT[n]))]
            for r in range(8):
                order += [("in_ga", r), ("in_gb", r), ("oa", r), ("ob", r)]
            order += [(n, i) for n in ("out", "gu", "down") for i in range(len(CAT[n]))]
            assert sorted(order) == sorted(SLAB_IDS.keys())
            for ci, (name, idx) in enumerate(order):
                sid = SLAB_IDS[(name, idx)]
                wname, k0, nkc, cols = CAT[name][idx]
                wdt = sum(c[1] for c in cols)
                dst = WB[sid, :, 0:nkc * wdt].rearrange("p (k c) -> p k c", c=wdt)
                off = 0
                for (c0, w) in cols:
                    src = W[wname][k0 * 128:(k0 + nkc) * 128, c0:c0 + w].rearrange("(k p) c -> p k c", p=128)
                    dma("pool", dst[:, :, off:off + w], src, reads=([(WMD, None)] if ci == 10 else []),
                        writes=[(WB, sid), (CVT[ci % 8], None)], dsem="cv%d" % (ci % 8))
                    off += w

        def build_gates(c):
            for gi, prm in enumerate((2, 5)):
                for blk in range(4):
                    pb = PB[nxt("pb", 4)]
                    for j in range(4):
                        m = blk * 4 + j
                        ts(tmpf[:, j * 128:(j + 1) * 128], ident_f[:], modfm[:, prm * 16 + m, c:c + 1], None, ALU.mult, None,
                           [(ident_f, None), (modfm, None)], [(tmpf, j)])
                        mm(pb[:, j * 128:(j + 1) * 128], ones_f[:], tmpf[:, j * 128:(j + 1) * 128], True, True,
                           [(ones_f, None), (tmpf, j)], [(pb, None)])
                    cp(gtb[gi][:, blk * 512:(blk + 1) * 512], pb[:], [(pb, None)], [(gtb[gi], blk)])

        def load_x(src_ap):
            dma("sp", xt[:], src_ap.rearrange("(t p) d -> p t d", p=128), writes=[(xt, None)], dsem="xt")

        def rstd_of(ss_ap, n, out_ap, rd, wr):
            act(out_ap, ss_ap, AF.Sqrt, list(rd) + [(epst, None)], wr, scale=1.0 / n, bias=epst[:, 0:1])
            S.op("dve", lambda e: e.reciprocal(out=out_ap, in_=out_ap), reads=wr, writes=wr)

        def norm_mod(c, which):
            xns = (xn, xn2)
            act(xn[:], xt[:, 0, :], AF.Square, [(xt, None)], [(xn, None), (small, 0)], accum_out=small[:, 0:1])
            S.op("dve", lambda e: e.scalar_tensor_tensor(out=xn2[:], in0=xt[:, 1, :], scalar=1.0, in1=xt[:, 1, :],
                                                         op0=ALU.mult, op1=ALU.mult, accum_out=small[:, 2:3]),
                 reads=[(xt, None)], writes=[(xn2, None), (small, 2)])
            rstd_of(small[:, 0:1], D, small[:, 1:2], [(small, 0)], [(small, 1)])
            rstd_of(small[:, 2:3], D, small[:, 3:4], [(small, 2)], [(small, 3)])
            ts(xn[:], xt[:, 0, :], small[:, 1:2], None, ALU.mult, None, [(xt, None), (small, 1)], [(xn, None)])
            act(xn2[:], xt[:, 1, :], AF.Copy, [(xt, None), (small, 3)], [(xn2, None)], scale=small[:, 3:4])
            for t_ in range(2):
                for half in range(2):
                    pr = nxt("pbt", 2)
                    bE, bO = NB[2 * pr], NB[2 * pr + 1]
                    for k in range(8):
                        kc = half * 8 + k
                        bk = bE if k % 2 == 0 else bO
                        sl = k // 2
                        tr(bk[:, sl * 128:(sl + 1) * 128], xns[t_][:, kc * 128:(kc + 1) * 128], ident_b[:],
                           [(xns[t_], None), (ident_b, None)], [(bk, sl)])
                    for k in range(8):
                        kc = half * 8 + k
                        sl = k // 2
                        if k % 2 == 0:
                            act(hT[:, kc, t_ * 128:(t_ + 1) * 128], bE[:, sl * 128:(sl + 1) * 128], AF.Identity,
                                [(bE, sl), (AB, None)], [(hT, (kc, t_))],
                                scale=AB[:, c, 2 * which, kc:kc + 1], bias=AB[:, c, 2 * which + 1, kc:kc + 1])
                        else:
                            ts(hT[:, kc, t_ * 128:(t_ + 1) * 128], bO[:, sl * 128:(sl + 1) * 128],
                               AB[:, c, 2 * which, kc:kc + 1], AB[:, c, 2 * which + 1, kc:kc + 1], ALU.mult, ALU.add,
                               [(bO, sl), (AB, None)], [(hT, (kc, t_))])

        def tm_matmul(name, nblk, lhs_of, nkc_total, evac):
            per_blk = len(CAT[name]) // nblk
            for blk in range(nblk):
                pbs = [PB[nxt("pb", 4)] for _ in range(2)]
                wdt = None
                kk = 0
                for si in range(per_blk):
                    slot, v = get_slab(name, blk * per_blk + si)
                    if S.dry:
                        continue
                    nkc, wdt = slab_view(CAT[name][blk * per_blk + si])
                    for ki in range(nkc):
                        for t_ in range(2):
                            lt, lap = lhs_of(kk, t_)
                            mm(pbs[t_][:, 0:wdt], lap, v[:, ki, :], kk == 0, kk == nkc_total - 1,
                               [(slot, None), lt], [(pbs[t_], None)])
                        kk += 1
                if not S.dry:
                    evac(blk, pbs, wdt)

        def kmax_update_all(kind, with_rope):
            sqa = U[:, 0:H * T]
            tt(sqa, ktst[:].rearrange("p h t -> p (h t)"), ktst[:].rearrange("p h t -> p (h t)"), ALU.mult,
               [(ktst, None)], [(U, "kvsq")])
            for pr in range(4):
                pb = PB[nxt("pb", 4)]
                mm(pb[:], ones_b[:], sqa[:, pr * 512:(pr + 1) * 512], True, True, [(ones_b, None), (U, "kvsq")], [(pb, None)])
                S.op("dve", lambda e, pb=pb, pr=pr: e.reduce_max(out=small[:, 12 + 2 * pr:14 + 2 * pr],
                                                                 in_=pb[:].rearrange("p (h t) -> p h t", t=T), axis=AX.X),
                     reads=[(pb, None)], writes=[(small, 12 + pr)])
            tt(kmax[kind][:], kmax[kind][:], small[:, 12:20], ALU.max,
               [(small, 12), (small, 13), (small, 14), (small, 15), (kmax[kind], None)], [(kmax[kind], None)])
            if with_rope:
                tt(sqr[:], krst[:], krst[:], ALU.mult, [(krst, None)], [(sqr, None)])
                pb = PB[nxt("pb", 4)]
                mm(pb[:, 0:T], ones_b[:], sqr[:], True, True, [(ones_b, None), (sqr, None)], [(pb, None)])
                S.op("dve", lambda e, pb=pb: e.reduce_max(out=small[:, 8:9], in_=pb[:, 0:T], axis=AX.X),
                     reads=[(pb, None)], writes=[(small, 8)])
                tt(kmaxr[:], kmaxr[:], small[:, 8:9], ALU.max, [(small, 8), (kmaxr, None)], [(kmaxr, None)])

        def kv_mla_from_tm(src_f32, src_dep, nt_tok0, rope, out_idx, KTd, KRd, Vd, koff, seq_idx):
            for t_ in range(2):
                cp(tmk[:, 0:320], src_f32(t_), [src_dep(t_)], [(tmk, 0)])
                pbt = PBT[nxt("pbt", 2)]
                for k in range(3):
                    tr(pbt[:, k * 128:(k + 1) * 128], tmk[:, k * 128:(k + 1) * 128], ident_b[:],
                       [(tmk, None), (ident_b, None)], [(pbt, k)])
                cp(ckvT[:, :, t_ * 128:(t_ + 1) * 128], pbt[:, 0:256].rearrange("p (k t) -> p k t", t=128),
                   [(pbt, None)], [(ckvT, t_)])
                cp(krst[:, t_ * 128:(t_ + 1) * 128], pbt[:, 256:384], [(pbt, None)], [(krst, t_)])
            kd = KRd[seq_idx] if seq_idx is not None else KRd[:, :]
            dma(store_q[0], kd[:, koff:koff + T] if seq_idx is None else kd, krst[:], reads=[(krst, None)],
                writes=[(KRd, (seq_idx, koff))], dsem="krst")
            for h in range(H):
                pb = PB[nxt("pb", 4)]
                for k in range(2):
                    mm(pb[:, 0:T], wuk[:, k, h * 128:(h + 1) * 128], ckvT[:, k, :], k == 0, k == 1,
                       [(wuk, None), (ckvT, None)], [(pb, None)])
                if h % 2 == 0:
                    act(ktst[:, h, :], pb[:, 0:T], AF.Copy, [(pb, None)], [(ktst, h)])
                else:
                    cp(ktst[:, h, :], pb[:, 0:T], [(pb, None)], [(ktst, h)])
            kmax_update_all("mla", True)
            if seq_idx is None:
                dst = KTd[:, :, koff:koff + T].rearrange("h p k -> p h k")
            else:
                dst = KTd[seq_idx].rearrange("h p k -> p h k")
            dma(store_q[0], dst, ktst[:], reads=[(ktst, None)], writes=[(KTd, (seq_idx, koff))], dsem="ktst")
            for t_ in range(2):
                for cb in range(2):
                    pb = PB[nxt("pb", 4)]
                    for k in range(2):
                        mm(pb[:], ckvT[:, k, t_ * 128:(t_ + 1) * 128], wuv[:, k, cb * 512:(cb + 1) * 512], k == 0, k == 1,
                           [(ckvT, None), (wuv, None)], [(pb, None)])
                    act(vst[:, t_, cb * 512:(cb + 1) * 512], pb[:], AF.Copy, [(pb, None)], [(vst, (t_, cb))])
            vd = Vd[seq_idx] if seq_idx is not None else Vd[koff:koff + T, :]
            dma(store_q[0], vd.rearrange("(t p) c -> p t c", p=128), vst[:], reads=[(vst, None)],
                writes=[(Vd, (seq_idx, koff))], dsem="vst")

        def kv_na_from_tm(k_bf_of, k_dep, v_bf_tile_written, KTd, Vd, koff, seq_idx):
            for t_ in range(2):
                pbt = PBT[nxt("pbt", 2)]
                for h in range(H):
                    tr(pbt[:, h * 128:(h + 1) * 128], k_bf_of(t_)[:, h * 128:(h + 1) * 128], ident_b[:],
                       [k_dep(t_), (ident_b, None)], [(pbt, h)])
                cp(ktst[:, :, t_ * 128:(t_ + 1) * 128], pbt[:].rearrange("p (h t) -> p h t", t=128),
                   [(pbt, None)], [(ktst, ("t", t_))])
            kmax_update_all("na", False)
            if seq_idx is None:
                dst = KTd[:, :, koff:koff + T].rearrange("h p k -> p h k")
            else:
                dst = KTd[seq_idx].rearrange("h p k -> p h k")
            dma(store_q[0], dst, ktst[:], reads=[(ktst, None)], writes=[(KTd, (seq_idx, koff))], dsem="ktst")
            vd = Vd[seq_idx] if seq_idx is not None else Vd[koff:koff + T, :]
            dma(store_q[0], vd.rearrange("(t p) c -> p t c", p=128), vst[:], reads=[(vst, None)],
                writes=[(Vd, (seq_idx, koff))], dsem="vst")

        def kv_latent(c, rope_tok0, prompt_idx, koff):
            def lhs(kk, t_):
                return (hT, (kk, t_)), hT[:, kk, t_ * 128:(t_ + 1) * 128]

            def evac(blk, pbs, wdt):
                for t_ in range(2):
                    pb = pbs[t_]
                    act(tmpf[:, 0:256], pb[:, 0:256], AF.Square, [(pb, None)], [(tmpf, None), (small, 2)],
                        accum_out=small[:, 2:3])
                    rstd_of(small[:, 2:3], KV_LORA, small[:, 3:4], [(small, 2)], [(small, 3)])
                    act(ckvst[:, t_, 0:256], pb[:, 0:256], AF.Copy, [(pb, None), (small, 3)], [(ckvst, t_)], scale=small[:, 3:4])
                    tt(ckvst[:, t_, 0:256], ckvst[:, t_, 0:256], kvgb[:], ALU.mult, [(ckvst, t_), (kvgb, None)], [(ckvst, t_)])
                    if rope_tok0 is None:
                        act(ckvst[:, t_, 256:320], pb[:, 256:320], AF.Copy, [(pb, None)], [(ckvst, t_)])
                    else:
                        x1, x2 = pb[:, 256:288], pb[:, 288:320]
                        co, si = ropt[:, t_, 0:32], ropt[:, t_, 32:64]
                        rd = [(pb, None), (ropt, None)]
                        tt(ropw[:, 0, :], x1, co, ALU.mult, rd, [(ropw, 0)])
                        tt(ropw[:, 1, :], x2, si, ALU.mult, rd, [(ropw, 1)])
                        tt(ropw[:, 2, :], x2, co, ALU.mult, rd, [(ropw, 2)])
                        tt(ropw[:, 3, :], x1, si, ALU.mult, rd, [(ropw, 3)])
                        tt(ckvst[:, t_, 256:288], ropw[:, 0, :], ropw[:, 1, :], ALU.subtract, [(ropw, 0), (ropw, 1)], [(ckvst, t_)])
                        tt(ckvst[:, t_, 288:320], ropw[:, 2, :], ropw[:, 3, :], ALU.add, [(ropw, 2), (ropw, 3)], [(ckvst, t_)])

            if rope_tok0 is not None:
                dma("sp", ropt[:], rope_tm[rope_tok0:rope_tok0 + T, :].rearrange("(t p) c -> p t c", p=128),
                    writes=[(ropt, None)], dsem="ropt")
            tm_matmul("in_kv", 1, lhs, KC, evac)
            if S.dry:
                return
            import os
            if int(os.environ.get("KSUB", "9")) < 2:
                return
            if prompt_idx is not None:
                dma(store_q[0], o_ckv[prompt_idx].rearrange("(t p) c -> p t c", p=128), ckvst[:, :, 0:256],
                    reads=[(ckvst, None)], dsem="ckvst")
                dma(store_q[0], o_kr[prompt_idx].rearrange("(t p) c -> p t c", p=128), ckvst[:, :, 256:320],
                    reads=[(ckvst, None)], dsem="ckvst")
                kv_mla_from_tm(lambda t_: ckvst[:, t_, :], lambda t_: (ckvst, t_), None, False, None,
                               KTm_p, KR_p, Vm_p, 0, prompt_idx)
            else:
                kv_mla_from_tm(lambda t_: ckvst[:, t_, :], lambda t_: (ckvst, t_), None, True, None,
                               KTm_s, KR_s, Vm_s, koff, None)

        def kv_na(prompt_idx, koff):
            def lhs(kk, t_):
                return (hT, (kk, t_)), hT[:, kk, t_ * 128:(t_ + 1) * 128]

            def evac_k(blk, pbs, wdt):
                for t_ in range(2):
                    if prompt_idx is not None:
                        act(ystage[:, blk * 512:(blk + 1) * 512], pbs[t_][:], AF.Copy, [(pbs[t_], None)], [(ystage, blk)])
                        dma(store_q[0], o_nak[prompt_idx, t_ * 128:(t_ + 1) * 128, blk * 512:(blk + 1) * 512],
                            ystage[:, blk * 512:(blk + 1) * 512], reads=[(ystage, blk)], dsem="ysk%d" % blk)
                    dst = (tmb if t_ == 0 else xn)
                    cp(dst[:, blk * 512:(blk + 1) * 512], pbs[t_][:], [(pbs[t_], None)], [(dst, blk)])

            def evac_v(blk, pbs, wdt):
                for t_ in range(2):
                    if prompt_idx is not None:
                        act(ystage[:, 1024 + blk * 512:1024 + (blk + 1) * 512], pbs[t_][:], AF.Copy,
                            [(pbs[t_], None)], [(ystage, 2 + blk)])
                        dma(store_q[0], o_nav[prompt_idx, t_ * 128:(t_ + 1) * 128, blk * 512:(blk + 1) * 512],
                            ystage[:, 1024 + blk * 512:1024 + (blk + 1) * 512], reads=[(ystage, 2 + blk)], dsem="ysv%d" % blk)
                    cp(vst[:, t_, blk * 512:(blk + 1) * 512], pbs[t_][:], [(pbs[t_], None)], [(vst, (t_, blk))])

            tm_matmul("in_kb", 2, lhs, KC, evac_k)
            tm_matmul("in_vb", 2, lhs, KC, evac_v)
            if S.dry:
                return
            kbf = lambda t_: (tmb if t_ == 0 else xn)
            if prompt_idx is not None:
                kv_na_from_tm(lambda t_: kbf(t_)[:, 0:1024], lambda t_: (kbf(t_), None), None, KTn_p, Vn_p, 0, prompt_idx)
            else:
                kv_na_from_tm(lambda t_: kbf(t_)[:, 0:1024], lambda t_: (kbf(t_), None), None, KTn_s, Vn_s, koff, None)

        def kv_ctx():
            dma("sp", ckvst[:, :, 0:256], c_ckv.rearrange("(t p) c -> p t c", p=128), writes=[(ckvst, None)], dsem="ckvst_l")
            dma("sp", ckvst[:, :, 256:320], c_kr.rearrange("(t p) c -> p t c", p=128), writes=[(ckvst, None)], dsem="ckvst_l")
            kv_mla_from_tm(lambda t_: ckvst[:, t_, :], lambda t_: (ckvst, None), None, False, None,
                           KTm_s, KR_s, Vm_s, 4096, None)
            for t_ in range(2):
                dst = (tmb if t_ == 0 else xn)
                dma("sp", ystage[:, 0:1024], c_nak[t_ * 128:(t_ + 1) * 128, :], writes=[(ystage, None)], dsem="ystage_l")
                cp(dst[:, 0:1024], ystage[:, 0:1024], [(ystage, None)], [(dst, None)])
                dma("sp", ystage[:, 1024:2048], c_nav[t_ * 128:(t_ + 1) * 128, :], writes=[(ystage, None)], dsem="ystage_l")
                cp(vst[:, t_, :], ystage[:, 1024:2048], [(ystage, None)], [(vst, (t_, None))])
            kbf = lambda t_: (tmb if t_ == 0 else xn)
            kv_na_from_tm(lambda t_: kbf(t_)[:, 0:1024], lambda t_: (kbf(t_), None), None, KTn_s, Vn_s, NA_ROWS * 64, None)

        def q_bias(kind, h, b, scale):
            pb = PB[nxt("pb", 4)]
            tt(sq[:], qn[b][:], qn[b][:], ALU.mult, [(qn[b], None)], [(sq, None)])
            mm(pb[:, 0:T], ones_b[:], sq[:], True, kind != "mla", [(ones_b, None), (sq, None)], [(pb, None)])
            if kind == "mla":
                tt(sqr[:], qr[b][:], qr[b][:], ALU.mult, [(qr[b], None)], [(sqr, None)])
                mm(pb[:, 0:T], ones_b[:], sqr[:], False, True, [(ones_b, None), (sqr, None)], [(pb, None)])
            S.op("dve", lambda e: e.reduce_max(out=small[:, 9:10], in_=pb[:, 0:T], axis=AX.X),
                 reads=[(pb, None)], writes=[(small, 9)])
            if kind == "mla":
                ts(small[:, 10:11], kmax[kind][:, h:h + 1], kmaxr[:, 0:1], scale * scale, ALU.add, ALU.mult,
                   [(kmax[kind], None), (kmaxr, None)], [(small, 10)])
            else:
                ts(small[:, 10:11], kmax[kind][:, h:h + 1], scale * scale, None, ALU.mult, None,
                   [(kmax[kind], None)], [(small, 10)])
            ts(bias_t[b][:, 0:1], small[:, 10:11], small[:, 9:10], -0.51 / scale, ALU.add, ALU.mult,
               [(small, 10), (small, 9)], [(bias_t[b], None)])

        def attn_block(b, nq0, nq, chunks, kt_parts, v_of, acc, first, last, out_ap, out_dep, extra=None):
            po, psu, c0 = acc
            n = len(chunks)
            bs = 512 // nq
            batches = [chunks[i:i + bs] for i in range(0, n, bs)]

            def qk(batch):
                pb = PB[nxt("pb", 4)]
                for bi_, c in enumerate(batch):
                    ex = extra(c) if extra else []
                    np_ = len(kt_parts) + len(ex)
                    j = 0
                    o_ = pb[:, bi_ * nq:(bi_ + 1) * nq]
                    for (lof, (rdep, rap)) in kt_parts:
                        ldep, lap = lof(c)
                        mm(o_, lap, rap, j == 0, j == np_ - 1, [ldep, rdep], [(pb, None)])
                        j += 1
                    for (ldep, lap, rdep, rap) in ex:
                        mm(o_, lap, rap, j == 0, j == np_ - 1, [ldep, rdep], [(pb, None)])
                        j += 1
                return pb

            def pv(batch, pb, idx0):
                p_ = pt[nxt("pt", 3)]
                w_ = len(batch) * nq
                act(p_[:, 0:w_], pb[:, 0:w_], AF.Exp, [(pb, None), (bias_t[b], None)], [(p_, None)], bias=bias_t[b][:, 0:1])
                for bi_, c in enumerate(batch):
                    i = idx0 + bi_
                    vdep, vap = v_of(c)
                    st_ = first and i == 0
                    sp_ = last and i == n - 1
                    r_ = p_[:, bi_ * nq:(bi_ + 1) * nq]
                    mm(po[:, c0:c0 + nq], vap, r_, st_, sp_, [vdep, (p_, None)], [(po, c0)])
                    mm(psu[:, c0:c0 + nq], ones_b[:], r_, st_, sp_, [(ones_b, None), (p_, None)], [(psu, c0)])

            pend = None
            idx = 0
            for batch in batches:
                pb = qk(batch)
                if pend is not None:
                    pv(*pend)
                pend = (batch, pb, idx)
                idx += len(batch)
            pv(*pend)
            if last:
                S.op("dve", lambda e: e.reciprocal(out=rinv[:, 0:nq], in_=psu[:, c0:c0 + nq]),
                     reads=[(psu, c0)], writes=[(rinv, None)])
                tt(out_ap, po[:, c0:c0 + nq], rinv[:, 0:nq], ALU.mult, [(po, c0), (rinv, None)], [out_dep])

        def load_kv_block(i, KTd, Vd, h, key0, nkeys, seq_idx, KRd=None, col0=0):
            kt_, v_ = ktb[i], vb[i]
            ksrc = KTd[seq_idx, h] if seq_idx is not None else KTd[h]
            vsrc = Vd[seq_idx] if seq_idx is not None else Vd[:, :]
            dma("sp", kt_[:, col0 * 128:col0 * 128 + nkeys], ksrc[:, key0:key0 + nkeys], reads=[(KTd, None)],
                writes=[(kt_, None)], dsem="ktb%d" % i)
            dma("sp", v_[:, col0:col0 + nkeys // 128, :],
                vsrc[key0:key0 + nkeys, h * 128:(h + 1) * 128].rearrange("(c p) d -> p c d", p=128),
                reads=[(Vd, None)], writes=[(v_, None)], dsem="vb%d" % i)
            if KRd is not None:
                rsrc = KRd[seq_idx] if seq_idx is not None else KRd[:, :]
                dma("sp", krb[i][:, col0 * 128:col0 * 128 + nkeys], rsrc[:, key0:key0 + nkeys], reads=[(KRd, None)],
                    writes=[(krb[i], None)], dsem="krb%d" % i)

        blk_rot = [0]

        def mla_attention(c, prompt_idx, tok0):
            if prompt_idx is None:
                dma("sp", ropf[:], rope_fm[:, :, tok0:tok0 + T], writes=[(ropf, None)], dsem="ropf")
            def proj(h):
                b = h % 2
                pbq = PB[nxt("pb", 4)]
                for k in range(4):
                    mm(pbq[:, 0:T], wuq[:, k, h * 192:h * 192 + 128], cqT[:, k, :], k == 0, k == 3, [(wuq, None), (cqT, None)], [(pbq, 0)])
                for k in range(4):
                    mm(pbq[:, T:2 * T], wuq[:, k, h * 192 + 128:h * 192 + 256], cqT[:, k, :], k == 0, k == 3, [(wuq, None), (cqT, None)], [(pbq, 1)])
                act(qn[b][:], pbq[:, 0:T], AF.Copy, [(pbq, 0)], [(qn[b], None)], scale=MLA_SCALE)
                if prompt_idx is not None:
                    act(qr[b][0:64, :], pbq[0:64, T:2 * T], AF.Copy, [(pbq, 1)], [(qr[b], None)], scale=MLA_SCALE)
                else:
                    pbr = PB[nxt("pb", 4)]
                    for k in range(4):
                        mm(pbr[:, 0:T], wuqr[:, k, h * 64:h * 64 + 128], cqT[:, k, :], k == 0, k == 3, [(wuqr, None), (cqT, None)], [(pbr, None)])
                    tt(tmpf[0:64, 0:T], pbq[0:64, T:2 * T], ropf[:, 0, :], ALU.mult, [(pbq, 1), (ropf, None)], [(tmpf, None)])
                    tt(tmpg[0:64, 0:T], pbr[0:64, 0:T], ropf[:, 1, :], ALU.mult, [(pbr, None), (ropf, None)], [(tmpg, None)])
                    tt(tmpf[0:64, 0:T], tmpf[0:64, 0:T], tmpg[0:64, 0:T], ALU.add, [(tmpf, None), (tmpg, None)], [(tmpf, None)])
                    act(qr[b][0:64, :], tmpf[0:64, 0:T], AF.Copy, [(tmpf, None)], [(qr[b], None)], scale=MLA_SCALE)
                q_bias("mla", h, b, MLA_SCALE)

            proj(0)
            for h in range(H):
                b = h % 2
                if h + 1 < H:
                    proj(h + 1)
                acc = ACC[h % 2] + (0,)
                if prompt_idx is not None:
                    blocks = [(0, SEQ)]
                else:
                    blocks = [(0, 1280), (1280, 1280), (2560, 1280), (3840, 512)]
                for bi, (key0, nkeys) in enumerate(blocks):
                    i = blk_rot[0]
                    blk_rot[0] = (i + 1) % 2
                    if prompt_idx is not None:
                        load_kv_block(i, KTm_p, Vm_p, h, key0, nkeys, prompt_idx, KR_p)
                    else:
                        load_kv_block(i, KTm_s, Vm_s, h, key0, nkeys, None, KR_s)
                    parts = [
                        (lambda c_, i=i: ((ktb[i], None), ktb[i][:, c_ * 128:(c_ + 1) * 128]), ((qn[b], None), qn[b][:])),
                        (lambda c_, i=i: ((krb[i], None), krb[i][:, c_ * 128:(c_ + 1) * 128]), ((qr[b], None), qr[b][:])),
                    ]
                    attn_block(b, 0, T, list(range(nkeys // 128)), parts,
                               lambda c_, i=i: ((vb[i], None), vb[i][:, c_, :]), acc,
                               bi == 0, bi == len(blocks) - 1, oaT(h), (U, ("oa", h)))

        def na_attention(prompt_idx, gi):
            if prompt_idx is None:
                dma("sp", mkt[0:2, :], mk_d[:, gi * 2 * 7 * 128:(gi + 1) * 2 * 7 * 128], writes=[(mkt, None)], dsem="mkt")
            cur = {}

            def proj(h):
                if h % 2 == 0:
                    cur["sv"] = get_slab("in_qb", h // 2)
                if S.dry:
                    return
                slot, v = cur["sv"]
                m = h % 2
                b = h % 2
                pbq = PB[nxt("pb", 4)]
                for kc in range(KC):
                    mm(pbq[:, 0:T], v[:, kc, m * 128:(m + 1) * 128], hT[:, kc, :], kc == 0, kc == KC - 1,
                       [(slot, None), (hT, None)], [(pbq, None)])
                act(qn[b][:], pbq[:, 0:T], AF.Copy, [(pbq, None)], [(qn[b], None)], scale=NA_SCALE)
                q_bias("na", h, b, NA_SCALE)

            proj(0)
            for h in range(H):
                b = h % 2
                if h + 1 < H:
                    proj(h + 1)
                if S.dry:
                    continue
                i = blk_rot[0]
                blk_rot[0] = (i + 1) % 2
                if prompt_idx is not None:
                    load_kv_block(i, KTn_p, Vn_p, h, 0, SEQ, prompt_idx)
                    parts = [(lambda c_, i=i: ((ktb[i], None), ktb[i][:, c_ * 128:(c_ + 1) * 128]), ((qn[b], None), qn[b][:]))]
                    attn_block(b, 0, T, [0, 1], parts, lambda c_, i=i: ((vb[i], None), vb[i][:, c_, :]),
                               ACC[h % 2] + (0,), True, True, obT(h), (U, ("ob", h)))
                else:
                    load_kv_block(i, KTn_s, Vn_s, h, gi * 2 * 128, 1024, None)
                    load_kv_block(i, KTn_s, Vn_s, h, NA_ROWS * 64, 256, None, col0=8)
                    for rp in range(2):
                        pl = 2 * gi + rp
                        jlo, jhi = -2, 2
                        if pl == 0:
                            jhi = 3
                        if pl == 15:
                            jlo = -3
                        wch = [rp + j + 3 for j in range(jlo, jhi + 1)]
                        qap = qn[b][:, rp * 128:(rp + 1) * 128]
                        parts = [(lambda c_, i=i: ((ktb[i], None), ktb[i][:, c_ * 128:(c_ + 1) * 128]), ((qn[b], None), qap))]

                        def extra(c_, rp=rp, h=h):
                            if c_ >= 8:
                                return []
                            j = c_ - rp
                            return [((tbt, None), tbt[:, j, h, :], (ident_b, None), ident_b[:]),
                                    ((e2, None), e2[:], (mkt, None), mkt[:, (rp * 7 + j) * 128:(rp * 7 + j + 1) * 128])]

                        attn_block(b, rp * 128, 128, wch + [8, 9], parts,
                                   lambda c_, i=i: ((vb[i], None), vb[i][:, c_, :]),
                                   ACC[h % 2] + (rp * 128,), True, True,
                                   obT(h)[:, rp * 128:(rp + 1) * 128], (U, ("ob", h, rp)), extra=extra)

        def cq_stage():
            pbs = [PB[nxt("pb", 4)] for _ in range(2)]
            for s2 in range(2):
                slot, v = get_slab("in_cq", s2)
                if S.dry:
                    continue
                for m in range(2):
                    for kc in range(KC):
                        mm(pbs[s2][:, m * T:(m + 1) * T], v[:, kc, m * 128:(m + 1) * 128], hT[:, kc, :], kc == 0, kc == KC - 1,
                           [(slot, None), (hT, None)], [(pbs[s2], m)])
            if S.dry:
                return
            pn = PB[nxt("pb", 4)]
            for j in range(4):
                src = pbs[j // 2][:, (j % 2) * T:(j % 2 + 1) * T]
                act(sq[:], src, AF.Square, [(pbs[j // 2], j % 2)], [(sq, None)])
                mm(pn[:, 0:T], ones_b[:], sq[:], j == 0, j == 3, [(ones_b, None), (sq, None)], [(pn, None)])
            act(rinv[:], pn[:, 0:T], AF.Sqrt, [(pn, None), (epst, None)], [(rinv, None)], scale=1.0 / Q_LORA, bias=epst[:, 0:1])
            S.op("dve", lambda e: e.reciprocal(out=rinv[:], in_=rinv[:]), reads=[(rinv, None)], writes=[(rinv, None)])
            for j in range(4):
                src = pbs[j // 2][:, (j % 2) * T:(j % 2 + 1) * T]
                tt(cqT[:, j, :], src, rinv[:], ALU.mult, [(pbs[j // 2], j % 2), (rinv, None)], [(cqT, j)])

        def merge_stage():
            for r in range(8):
                banks = [PB[k] for k in range(4)]
                specs = [("in_ga", KC, lambda kc: ((hT, None), hT[:, kc, :])),
                         ("in_gb", KC, lambda kc: ((hT, None), hT[:, kc, :])),
                         ("oa", 8, lambda kc: ((U, ("oa", kc)), oaT(kc))),
                         ("ob", 8, lambda kc: ((U, ("ob", kc)), obT(kc)))]
                for bi, (name, nk, rhs_of) in enumerate(specs):
                    slot, v = get_slab(name, r)
                    if S.dry:
                        continue
                    for m in range(2):
                        for kc in range(nk):
                            rdep, rap = rhs_of(kc)
                            rd = [(slot, None), rdep]
                            if name == "ob":
                                rd = [(slot, None), (U, ("ob", kc)), (U, ("ob", kc, 0)), (U, ("ob", kc, 1))]
                            mm(banks[bi][:, m * T:(m + 1) * T], v[:, kc, m * 128:(m + 1) * 128], rap, kc == 0, kc == nk - 1,
                               rd, [(banks[bi], m)])
                if S.dry:
                    continue
                act(tmpf[:], banks[0][:], AF.Sigmoid, [(banks[0], None)], [(tmpf, None)])
                act(tmpg[:], banks[1][:], AF.Sigmoid, [(banks[1], None)], [(tmpg, None)])
                tt(tmpf[:], banks[2][:], tmpf[:], ALU.mult, [(banks[2], None), (tmpf, None)], [(tmpf, None)])
                tt(tmpg[:], banks[3][:], tmpg[:], ALU.mult, [(banks[3], None), (tmpg, None)], [(tmpg, None)])
                for m in range(2):
                    tt(mgT(2 * r + m), tmpf[:, m * T:(m + 1) * T], tmpg[:, m * T:(m + 1) * T], ALU.add,
                       [(tmpf, None), (tmpg, None)], [(U, ("mg", 2 * r + m))])

        def resid_tm(name, lhs_of, nkc_total, gi_):
            def evac(blk, pbs, wdt):
                for t_ in range(2):
                    tt(tmpf[:], pbs[t_][:], gtb[gi_][:, blk * 512:(blk + 1) * 512], ALU.mult,
                       [(pbs[t_], None), (gtb[gi_], None)], [(tmpf, None)])
                    tt(xt[:, t_, blk * 512:(blk + 1) * 512], xt[:, t_, blk * 512:(blk + 1) * 512], tmpf[:], ALU.add,
                       [(xt, None), (tmpf, None)], [(xt, None)])
            tm_matmul(name, 4, lhs_of, nkc_total, evac)

        def ffn_stage():
            for j in range(NFF):
                slot, v = get_slab("gu", j)
                if S.dry:
                    continue
                pb = PB[nxt("pb", 4)]
                for m in range(2):
                    for kc in range(KC):
                        mm(pb[:, m * T:(m + 1) * T], v[:, kc, m * 128:(m + 1) * 128], hT[:, kc, :], kc == 0, kc == KC - 1,
                           [(slot, None), (hT, None)], [(pb, m)])
                act(tmpf[:, 0:T], pb[:, 0:T], AF.Silu, [(pb, 0)], [(tmpf, None)])
                tt(actT(j), tmpf[:, 0:T], pb[:, T:2 * T], ALU.mult, [(tmpf, None), (pb, 1)], [(U, ("act", j))])

        def final_out(dst_ap):
            for t_ in range(2):
                act(xn[:], xt[:, t_, :], AF.Square, [(xt, None)], [(xn, None), (small, 4)], accum_out=small[:, 4:5])
                rstd_of(small[:, 4:5], D, small[:, 5:6], [(small, 4)], [(small, 5)])
                act(ystage[:], xt[:, t_, :], AF.Copy, [(xt, None), (small, 5)], [(ystage, None)], scale=small[:, 5:6])
                tt(ystage[:], ystage[:], gfb[:], ALU.mult, [(ystage, None), (gfb, None)], [(ystage, None)])
                dma("pool", dst_ap[t_ * 128:(t_ + 1) * 128, :], ystage[:], reads=[(ystage, None)], dsem="ystage")

        def u_barrier():
            S.op("dve", lambda e: e.memset(small[:, 20:21], 0.0), writes=[(U, None), (small, 20)])

        def main_group(c, src_ap, dst_ap, prompt_idx, gi):
            load_x(src_ap)
            norm_mod(c, 0)
            u_barrier()
            cq_stage()
            mla_attention(c, prompt_idx, None if prompt_idx is not None else gi * T)
            na_attention(prompt_idx, gi)
            merge_stage()
            resid_tm("out", lambda kk, t_: ((U, ("mg", kk)), mgT(kk)[:, t_ * 128:(t_ + 1) * 128]), KC, 0)
            norm_mod(c, 1)
            u_barrier()
            ffn_stage()
            resid_tm("down", lambda kk, t_: ((U, ("act", kk)), actT(kk)[:, t_ * 128:(t_ + 1) * 128]), NFF, 1)
            final_out(dst_ap)

        def program():
            if not S.dry:
                prologue()
                if stage >= 1:
                    convert_weights()
            if stage < 3:
                return
            jobs = [("p", s_) for s_ in range(4)]
            if stage >= 6:
                jobs += [("m", g) for g in range(16)] + [("n", g) for g in range(NA_ROWS // 4)]

            def src_of(j):
                if j[0] == "p":
                    return xp[j[1]]
                if j[0] == "m":
                    return xs_full[j[1] * T:(j[1] + 1) * T, :]
                return xs_na[j[1] * T:(j[1] + 1) * T, :]

            if not S.dry:
                load_x(src_of(jobs[0]))
            for i, j in enumerate(jobs):
                if not S.dry:
                    if i == 4:
                        kv_ctx()
                    norm_mod(0 if j[0] == "p" else 1, 0)
                    if i + 1 < len(jobs):
                        load_x(src_of(jobs[i + 1]))
                if j[0] == "p":
                    kv_latent(0, None, j[1], 0)
                    kv_na(j[1], 0)
                elif j[0] == "m":
                    kv_latent(1, j[1] * T, None, j[1] * T)
                else:
                    kv_na(None, j[1] * T)
            if stage < 6:
                return
            store_q[0] = "pool"
            if not S.dry:
                build_gates(0)
            for s_ in range(4):
                main_group(0, xp[s_], yp[s_], s_, None)
            if not S.dry:
                build_gates(1)
            for g in range(8):
                main_group(1, xs_own[g * T:(g + 1) * T, :], ys[g * T:(g + 1) * T, :], None, g)

        S.dry = True
        program()
        S.dry = False
        store_q[0] = "sp"
        stream["pos"] = 0
        program()
        S.emit(st)
        print("ops:", {e: len(S.ops[e]) for e in ENGS}, "dsems:", len(S.dsem_counts), flush=True)
    return nc


def _rope_tables():
    t = np.arange(4096)
    row = (t // GRID_W).astype(np.float32)
    col = (t % GRID_W).astype(np.float32)
    nf = ROPE // 4
    inv = (np.float32(10000.0) ** (-np.arange(nf, dtype=np.float32) / nf)).astype(np.float32)
    ang = np.concatenate([row[:, None] * inv, col[:, None] * inv], axis=-1).astype(np.float32)
    return np.cos(ang).astype(np.float32), np.sin(ang).astype(np.float32)


def _na_tables(rpb):
    qc = np.arange(64)
    kc = np.arange(64)
    q_start = np.clip(qc - 8, 0, 48)
    inw = (kc[None, :] >= q_start[:, None]) & (kc[None, :] < q_start[:, None] + 16)
    coff = np.clip(kc[None, :] - qc[:, None] + 15, 0, 30)
    tb = np.zeros((2, 64, 7, H, 2, 64), np.float32)
    for j in range(7):
        for qro in range(2):
            for kro in range(2):
                dr = 2 * (j - 3) + kro - qro
                if abs(dr) > 7:
                    continue
                g = rpb[:, dr + 7, :][:, coff]
                g = np.where(inw[None], g, np.float32(NEG))
                tb[qro, :, j, :, kro, :] = np.transpose(g, (1, 0, 2))
    return tb.reshape(128, 7 * H * 128)


def _na_masks(half):
    mk = np.zeros((2, 16, 7, 2, 64), np.float32)
    for pl in range(16):
        P = pl + 16 * half
        for j in range(7):
            p = P + j - 3
            for kro in range(2):
                kr = 2 * p + kro
                for qro in range(2):
                    r = 2 * P + qro
                    r0 = min(max(r - 4, 0), 56)
                    ok = (0 <= kr < 64) and (r0 <= kr < r0 + 8)
                    mk[kro, pl, j, qro, :] = 0.0 if ok else NEG
    return mk.reshape(2, 16 * 7 * 128).astype(ml_dtypes.bfloat16)


_NC_CACHE = {}


def kernel(x_prompt, x_sample, cache_mla_ckv, cache_mla_krope, cache_na_k, cache_na_v, c, c_ctx,
           w_mod, b_mod, norm1_g, w_in, q_norm_g, kv_norm_g, w_uq, w_uk, w_uv, rpb,
           w_oa, w_ob, w_out, norm2_g, w_gu, w_down, norm_f_g):
    f = lambda a: np.ascontiguousarray(np.asarray(a, dtype=np.float32))
    x_prompt, x_sample = f(x_prompt), f(x_sample)
    if "nc" not in _NC_CACHE:
        _NC_CACHE["nc"] = build_program()
    nc = _NC_CACHE["nc"]
    cos, sin = _rope_tables()
    rope_tm = np.concatenate([cos, sin], axis=1)
    fm = lambda v, n: f(v).reshape(n, 128).T
    tb = _na_tables(f(rpb)[0])
    e2 = np.zeros((128, 128), np.float32)
    e2[0, :64] = 1.0
    e2[1, 64:] = 1.0
    shared = {
        "w_in": f(w_in)[0], "w_oa": f(w_oa)[0], "w_ob": f(w_ob)[0], "w_out": f(w_out)[0], "w_gu": f(w_gu)[0],
        "w_down": f(w_down)[0], "w_mod": f(w_mod)[0], "w_uq": f(w_uq)[0], "w_uk": f(w_uk)[0], "w_uv": f(w_uv)[0],
        "kvg_b": np.ascontiguousarray(np.broadcast_to(f(kv_norm_g)[0][None, :], (128, KV_LORA))),
        "gf_b": np.ascontiguousarray(np.broadcast_to(f(norm_f_g)[None, :], (128, D))),
        "ident": np.eye(128, dtype=np.float32), "rope_tm": rope_tm, "tb": tb,
        "e2": e2.astype(ml_dtypes.bfloat16),
    }
    in_maps = []
    for core in range(8):
        b, half = core // 2, core % 2
        cond = np.stack([fm(c_ctx, 16), fm(f(c)[b], 16)], axis=-1).reshape(128, 32)
        vecs = np.concatenate([cond, fm(f(norm1_g)[0], 16), fm(f(norm2_g)[0], 16), fm(f(q_norm_g)[0], 4),
                               fm(f(b_mod)[0], 96)], axis=1)
        xs = x_sample[b]
        xna = np.zeros((NA_ROWS * 64, D), np.float32)
        if half == 0:
            xna[6 * 64:] = xs[0:38 * 64]
        else:
            xna[:38 * 64] = xs[26 * 64:]
        own = slice(half * 2048, (half + 1) * 2048)
        rfm = np.stack([np.concatenate([cos[own].T, cos[own].T], 0), np.concatenate([sin[own].T, sin[own].T], 0)], axis=1)
        m = dict(shared)
        m.update({
            "xp": x_prompt[core * 4:(core + 1) * 4], "xs_full": xs, "xs_na": xna, "xs_own": np.ascontiguousarray(xs[own]),
            "c_ckv": f(cache_mla_ckv)[b, 0], "c_kr": f(cache_mla_krope)[b, 0],
            "c_nak": f(cache_na_k)[b, 0].reshape(256, H * HD), "c_nav": f(cache_na_v)[b, 0].reshape(256, H * HD),
            "vecs": np.ascontiguousarray(vecs), "rope_fm": np.ascontiguousarray(rfm.astype(np.float32)),
            "mk": _na_masks(half),
        })
        in_maps.append(m)
    if _NC_CACHE.get("debug_hook") is not None:
        return _NC_CACHE["debug_hook"](in_maps)
    res = run_bass_kernel_spmd(nc, in_maps, core_ids=list(range(8))).results
    y_prompt = np.concatenate([r["yp"] for r in res], axis=0)
    y_sample = np.stack([np.concatenate([res[2 * b]["ys"], res[2 * b + 1]["ys"]], axis=0) for b in range(4)], axis=0)
    ckv = np.concatenate([r["o_ckv"] for r in res], axis=0)[:, None]
    kr = np.concatenate([r["o_kr"] for r in res], axis=0)[:, None]
    nak = np.concatenate([r["o_nak"] for r in res], axis=0).reshape(32, 1, SEQ, H, HD)
    nav = np.concatenate([r["o_nav"] for r in res], axis=0).reshape(32, 1, SEQ, H, HD)
    return (y_prompt.astype(np.float32), y_sample.astype(np.float32), ckv.astype(np.float32), kr.astype(np.float32),
            nak.astype(np.float32), nav.astype(np.float32))
```
